# Optimizing a Trainium2 kernel written in Bass

```python
import jax, jax.numpy as jnp
from jax import lax
import numpy as np

D_MODEL = 1024
BATCH = 32
SEQ = 2048
DEPTH = 1

CTX_LEN = 256
GRID_W = 64
N_SUB = 3
N_MOD = 3 * N_SUB
FFN_HIDDEN = 2816
GLA_HEADS = 4
GLA_DK = 128
GLA_DV = 256
GLA_KEY = GLA_HEADS * GLA_DK
GLA_VAL = GLA_HEADS * GLA_DV
GLA_RANK = 16
GLA_TAU = 16.0
GLA_CHUNK = 64
LRU_WIDTH = 1024
LRU_BLOCKS = 8
LRU_BW = LRU_WIDTH // LRU_BLOCKS
LRU_C = 8.0
CONV_K = 4
EPS = 1e-6
IN_SIZES = (GLA_KEY, GLA_KEY, GLA_VAL, GLA_VAL, GLA_RANK, GLA_RANK,
            LRU_WIDTH, LRU_WIDTH, D_MODEL, D_MODEL)
IN_WIDTH = 2 * GLA_KEY + 2 * GLA_VAL + 2 * GLA_RANK + 2 * LRU_WIDTH + 2 * D_MODEL

kernel_name = "hybrid_gla_rglru_macaron_prefix_block"


def rmsnorm(x, w):
    xf = x.astype(jnp.float32)
    y = xf * lax.rsqrt(jnp.mean(xf * xf, axis=-1, keepdims=True) + EPS)
    return (y * w.astype(jnp.float32)).astype(x.dtype)


def modulate(h, w, shift, scale):
    return rmsnorm(h, w) * (1.0 + scale) + shift


def swiglu(u, wi, wo):
    gate, up = jnp.split(u @ wi, 2, axis=-1)
    return (jax.nn.silu(gate) * up) @ wo


def ffn_half(t, m, s, nw, wi, wo):
    u = modulate(t, nw, m[3 * s], m[3 * s + 1])
    return t + 0.5 * m[3 * s + 2] * swiglu(u, wi, wo)


def flip(t):
    return jnp.flip(t, axis=1)


def dwconv_centred(t, w, b):
    L = t.shape[1]
    left = CONV_K // 2
    tp = jnp.pad(t, ((0, 0), (left, CONV_K - 1 - left), (0, 0)))
    out = b
    for j in range(CONV_K):
        out = out + tp[:, j:j + L] * w[j]
    return out


def gla_chunk(q, k, v, log_a, s0, strict):
    Bn, T, H, DK = q.shape
    DV = v.shape[-1]
    n = T // GLA_CHUNK
    q = q.reshape(Bn, n, GLA_CHUNK, H, DK)
    k = k.reshape(Bn, n, GLA_CHUNK, H, DK)
    v = v.reshape(Bn, n, GLA_CHUNK, H, DV)
    b = jnp.cumsum(log_a.reshape(Bn, n, GLA_CHUNK, H, DK), axis=2)
    b_last = b[:, :, -1:]
    q_dec = q * jnp.exp(b)
    k_inv = k * jnp.exp(-b)
    k_end = k * jnp.exp(b_last - b)
    scores = jnp.einsum('bnihd,bnjhd->bnhij', q_dec, k_inv)
    mask = jnp.tril(jnp.ones((GLA_CHUNK, GLA_CHUNK), dtype=bool), -1 if strict else 0)
    scores = jnp.where(mask, scores, 0.0)
    o_intra = jnp.einsum('bnhij,bnjhe->bnihe', scores, v)

    def step(s, inp):
        qd, ke, vv, bl = inp
        o = jnp.einsum('bihd,bhde->bihe', qd, s)
        s = s * jnp.exp(bl)[..., None] + jnp.einsum('bjhd,bjhe->bhde', ke, vv)
        return s, o

    xs = (jnp.moveaxis(q_dec, 1, 0), jnp.moveaxis(k_end, 1, 0),
          jnp.moveaxis(v, 1, 0), jnp.moveaxis(b_last[:, :, 0], 1, 0))
    s_fin, o_inter = lax.scan(step, s0, xs)
    o = o_intra + jnp.moveaxis(o_inter, 0, 1)
    return o.reshape(Bn, T, H, DV), s_fin


def rglru_scan(xc, w_r, b_r, w_i, b_i, lam, h0):
    Bn, T, W = xc.shape
    xb = xc.reshape(Bn, T, LRU_BLOCKS, LRU_BW)
    r = jax.nn.sigmoid(jnp.einsum('btnc,ncd->btnd', xb, w_r).reshape(Bn, T, W) + b_r)
    i = jax.nn.sigmoid(jnp.einsum('btnc,ncd->btnd', xb, w_i).reshape(Bn, T, W) + b_i)
    log_a = -LRU_C * r * jax.nn.softplus(-lam)
    a = jnp.exp(log_a)
    u = jnp.sqrt(-jnp.expm1(2.0 * log_a)) * (i * xc)

    def combine(e1, e2):
        a1, b1 = e1
        a2, b2 = e2
        return a1 * a2, a2 * b1 + b2

    a_cum, h_zero = lax.associative_scan(combine, (a, u), axis=1)
    h = a_cum * h0[:, None] + h_zero
    return h, h[:, -1]


def token_mixer(proj, conv_fn, init, lp, need_out):
    (fup, fb, gnorm_w, conv_w, conv_b, lru_wr, lru_br, lru_wi, lru_bi, lru_lam,
     w_out_gla, w_out_lru, w_o) = lp
    dt = proj.dtype
    split_idx = [int(s) for s in np.cumsum(IN_SIZES)[:-1]]
    q, k, v, g, fd_f, fd_b, xl, yl, ga, gb = jnp.split(proj, split_idx, axis=-1)
    Bn, T, _ = proj.shape
    f32 = jnp.float32

    def heads(t, d):
        return t.astype(f32).reshape(Bn, T, GLA_HEADS, d)

    qh = heads(q, GLA_DK) * (GLA_DK ** -0.5)
    kh = heads(k, GLA_DK)
    vh = heads(v, GLA_DV)
    la_f = heads(jax.nn.log_sigmoid(fd_f @ fup[0] + fb[0]), GLA_DK) / GLA_TAU
    la_b = heads(jax.nn.log_sigmoid(fd_b @ fup[1] + fb[1]), GLA_DK) / GLA_TAU
    s_f0, s_b0, h_f0, h_b0 = init
    o_f, s_f = gla_chunk(qh, kh, vh, la_f, s_f0, False)
    o_b, s_b = gla_chunk(flip(qh), flip(kh), flip(vh), flip(la_b), s_b0, True)

    xc = conv_fn(xl, conv_w, conv_b).astype(f32)
    hf, h_f = rglru_scan(xc, lru_wr[0], lru_br[0], lru_wi[0], lru_bi[0], lru_lam[0], h_f0)
    hb, h_b = rglru_scan(flip(xc), lru_wr[1], lru_br[1], lru_wi[1], lru_bi[1], lru_lam[1], h_b0)
    states = (s_f, s_b, h_f, h_b)
    if not need_out:
        return None, states

    o = o_f + flip(o_b)
    o = rmsnorm(o, gnorm_w) * jax.nn.silu(heads(g, GLA_DV))
    y_gla = o.reshape(Bn, T, GLA_VAL).astype(dt) @ w_out_gla
    y_lru = ((hf + flip(hb)).astype(dt) * jax.nn.gelu(yl)) @ w_out_lru
    merged = jax.nn.sigmoid(ga) * y_gla + jax.nn.sigmoid(gb) * y_lru
    return merged @ w_o, states


def setup_inputs(seed: int = 0) -> dict:
    key = jax.random.key(seed)
    ks = jax.random.split(key, 32)
    nrm = jax.random.normal
    D = D_MODEL
    p = (jax.random.uniform(ks[25], (DEPTH, 2, LRU_WIDTH), minval=0.9, maxval=0.999)) ** (1.0 / LRU_C)
    return {
        "x": nrm(ks[0], (BATCH, SEQ, D)),
        "c": nrm(ks[1], (BATCH, D)),
        "ctx": nrm(ks[2], (BATCH, CTX_LEN, D)),
        "c_ctx": nrm(ks[3], (D,)),
        "w_ada": nrm(ks[4], (DEPTH, D, N_MOD * D)) * (0.5 * D ** -0.5),
        "b_ada": nrm(ks[5], (DEPTH, N_MOD * D)) * 0.02,
        "norm_w": 1.0 + 0.05 * nrm(ks[6], (DEPTH, N_SUB, D)),
        "ffn1_wi": nrm(ks[7], (DEPTH, D, 2 * FFN_HIDDEN)) * D ** -0.5,
        "ffn1_wo": nrm(ks[8], (DEPTH, FFN_HIDDEN, D)) * FFN_HIDDEN ** -0.5,
        "ffn2_wi": nrm(ks[9], (DEPTH, D, 2 * FFN_HIDDEN)) * D ** -0.5,
        "ffn2_wo": nrm(ks[10], (DEPTH, FFN_HIDDEN, D)) * FFN_HIDDEN ** -0.5,
        "w_in": nrm(ks[11], (DEPTH, D, IN_WIDTH)) * D ** -0.5,
        "gla_fup": nrm(ks[12], (DEPTH, 2, GLA_RANK, GLA_KEY)) * GLA_RANK ** -0.5,
        "gla_fb": nrm(ks[13], (DEPTH, 2, GLA_KEY)) * 0.1,
        "gla_norm_w": 1.0 + 0.05 * nrm(ks[14], (DEPTH, GLA_DV)),
        "conv_w": nrm(ks[15], (DEPTH, CONV_K, LRU_WIDTH)) * CONV_K ** -0.5,
        "conv_b": nrm(ks[16], (DEPTH, LRU_WIDTH)) * 0.02,
        "lru_wr": nrm(ks[17], (DEPTH, 2, LRU_BLOCKS, LRU_BW, LRU_BW)) * LRU_BW ** -0.5,
        "lru_br": nrm(ks[18], (DEPTH, 2, LRU_WIDTH)) * 0.02,
        "lru_wi": nrm(ks[19], (DEPTH, 2, LRU_BLOCKS, LRU_BW, LRU_BW)) * LRU_BW ** -0.5,
        "lru_bi": nrm(ks[20], (DEPTH, 2, LRU_WIDTH)) * 0.02,
        "lru_lam": jnp.log(p) - jnp.log1p(-p),
        "w_out_gla": nrm(ks[21], (DEPTH, GLA_VAL, D)) * GLA_VAL ** -0.5,
        "w_out_lru": nrm(ks[22], (DEPTH, LRU_WIDTH, D)) * LRU_WIDTH ** -0.5,
        "w_o": nrm(ks[23], (DEPTH, D, D)) * D ** -0.5,
        "final_norm_w": 1.0 + 0.05 * nrm(ks[24], (D,)),
    }


def reference(x, c, ctx, c_ctx, w_ada, b_ada, norm_w, ffn1_wi, ffn1_wo, ffn2_wi, ffn2_wo,
              w_in, gla_fup, gla_fb, gla_norm_w, conv_w, conv_b, lru_wr, lru_br, lru_wi,
              lru_bi, lru_lam, w_out_gla, w_out_lru, w_o, final_norm_w):
    Bn, T, _ = x.shape
    rows = T // GRID_W

    def lat_conv(t, w, b):
        return dwconv_centred(t.reshape(Bn * rows, GRID_W, t.shape[-1]), w, b).reshape(Bn, T, t.shape[-1])

    f32 = jnp.float32
    zero_init = (jnp.zeros((Bn, GLA_HEADS, GLA_DK, GLA_DV), f32),
                 jnp.zeros((Bn, GLA_HEADS, GLA_DK, GLA_DV), f32),
                 jnp.zeros((Bn, LRU_WIDTH), f32),
                 jnp.zeros((Bn, LRU_WIDTH), f32))
    h, hc = x, ctx
    for l in range(DEPTH):
        last = l == DEPTH - 1
        mod = (jax.nn.silu(c) @ w_ada[l] + b_ada[l]).reshape(Bn, N_MOD, 1, D_MODEL)
        mod_c = (jax.nn.silu(c_ctx) @ w_ada[l] + b_ada[l]).reshape(N_MOD, D_MODEL)
        lat_m = [mod[:, j] for j in range(N_MOD)]
        ctx_m = [mod_c[j] for j in range(N_MOD)]
        lp = (gla_fup[l], gla_fb[l], gla_norm_w[l], conv_w[l], conv_b[l], lru_wr[l], lru_br[l],
              lru_wi[l], lru_bi[l], lru_lam[l], w_out_gla[l], w_out_lru[l], w_o[l])

        h = ffn_half(h, lat_m, 0, norm_w[l, 0], ffn1_wi[l], ffn1_wo[l])
        hc = ffn_half(hc, ctx_m, 0, norm_w[l, 0], ffn1_wi[l], ffn1_wo[l])

        uc = modulate(hc, norm_w[l, 1], ctx_m[3], ctx_m[4]) @ w_in[l]
        yc, ctx_states = token_mixer(uc, dwconv_centred, zero_init, lp, not last)
        u = modulate(h, norm_w[l, 1], lat_m[3], lat_m[4]) @ w_in[l]
        y, _ = token_mixer(u, lat_conv, ctx_states, lp, True)
        h = h + lat_m[5] * y

        h = ffn_half(h, lat_m, 2, norm_w[l, 2], ffn2_wi[l], ffn2_wo[l])
        if not last:
            hc = hc + ctx_m[5] * yc
            hc = ffn_half(hc, ctx_m, 2, norm_w[l, 2], ffn2_wi[l], ffn2_wo[l])
    return rmsnorm(h, final_norm_w)
```

```python
import numpy as np
from contextlib import ExitStack
import concourse.bass as bass
import concourse.mybir as mybir
from concourse.bass_utils import run_bass_kernel_spmd

F32 = mybir.dt.float32
BF16 = mybir.dt.bfloat16
AF = mybir.ActivationFunctionType
ALU = mybir.AluOpType

D = 1024
KC = 8
FH = 2816
HC = 22
EPS = 1e-6
NSLOT = 4
WIN_BLK = {"q": [0], "k": [512], "v": [1024, 1536], "g": [2048, 2560], "xl": [3104, 3616],
           "yl": [4128, 4640], "ga": [5152, 5664], "gb": [6176, 6688]}
WIN_ORDER = ["q", "k", "v", "g", "xl", "yl", "ga", "gb"]
WIN_IDX = {}
_i = 0
for _n in WIN_ORDER:
    for _j in range(len(WIN_BLK[_n])):
        WIN_IDX[(_n, _j)] = _i
        _i += 1
N_WIN_BLK = _i

VC = {}
_o = 0
for _n, _w in [("nw", 24), ("bada", 72), ("cw", 32), ("cb", 8), ("lbr", 16), ("lbi", 16), ("lam", 16),
               ("gnw", 2), ("fnw", 8)]:
    VC[_n] = _o
    _o += _w
NVEC = _o
CC = {"tri0": 0, "tri1": 128, "utri0": 256, "utri1": 384, "mask0": 512, "mask1": 640}
NCONST = 768


class Eng:
    def __init__(self, name, b, sem):
        self.name, self.b, self.sem, self.cnt, self.seen = name, b, sem, 0, {}


class Chan:
    def __init__(self, sem):
        self.sem, self.cnt = sem, 0


class Res:
    __slots__ = ("w", "r")

    def __init__(self):
        self.w = None
        self.r = {}


class ResGroup:
    def __init__(self, n):
        self.parts = [Res() for _ in range(n)]


def _flat(rs):
    out = []
    for r in rs:
        if isinstance(r, ResGroup):
            out.extend(r.parts)
        else:
            out.append(r)
    return out


class Prog:
    def __init__(self, NB, T, TC, TW=512):
        self.NB, self.T, self.TC, self.TW = NB, T, TC, TW
        self.NBC = NB + 1
        self.nc = bass.Bass("TRN2", target_bir_lowering=False)
        self.es = ExitStack()
        self.nsem = 0
        self.dres = {}
        self.sp_jobs = []

    def sem(self):
        self.nsem += 1
        return self.es.enter_context(self.nc.semaphore("s%d" % self.nsem))

    def chan(self):
        return Chan(self.sem())

    def sb(self, name, shape, dt, st=None):
        self.nsb = getattr(self, "nsb", 0) + 1
        stack = st if st is not None else self.es
        return stack.enter_context(self.nc.sbuf_tensor("%s_%d" % (name, self.nsb), shape, dt))

    def dr(self, key):
        r = self.dres.get(key)
        if r is None:
            r = self.dres[key] = Res()
        return r

    def _need(self, eng, reads, writes):
        reads, writes = _flat(reads), _flat(writes)
        need = {}

        def add(sc):
            if sc is None:
                return
            s, c = sc
            if need.get(s, 0) < c:
                need[s] = c
        for r in reads:
            add(r.w)
        for w in writes:
            add(w.w)
            for s, c in w.r.items():
                add((s, c))
        out = []
        for s, c in need.items():
            if s is eng and eng.name == "pe":
                continue
            if eng.seen.get(s, 0) >= c:
                continue
            eng.seen[s] = c
            out.append((s, c))
        return out

    def _waits(self, eng, reads, writes):
        for s, c in self._need(eng, reads, writes):
            eng.b.wait_ge(s.sem, c)

    @staticmethod
    def _commit(src, cnt, reads, writes):
        reads, writes = _flat(reads), _flat(writes)
        for r in reads:
            r.r[src] = cnt
        for w in writes:
            w.w = (src, cnt)
            w.r = {}

    def op(self, eng, fn, reads, writes):
        self._waits(eng, reads, writes)
        ins = fn()
        eng.cnt += 1
        ins.then_inc(eng.sem, 1)
        self._commit(eng, eng.cnt, reads, writes)

    def mm(self, ps_res, items, reads):
        pe = self.pe
        self._waits(pe, reads, [ps_res])
        n = len(items)
        ins = None
        for i, (o, l, r, st) in enumerate(items):
            stop = (i == n - 1) or items[i + 1][3]
            ins = self.nc.tensor.matmul(o, lhsT=l, rhs=r, start=st, stop=stop)
        pe.cnt += 1
        ins.then_inc(pe.sem, 1)
        self._commit(pe, pe.cnt, reads, [ps_res])

    def mmacc(self, ps_res, out, pairs, reads):
        self.mm(ps_res, [(out, l, r, i == 0) for i, (l, r) in enumerate(pairs)], reads)

    def dma(self, q, chan, out, in_, reads, writes):
        self._waits(q, reads, writes)
        q.b.dma_start(out=out, in_=in_).then_inc(chan.sem, 16)
        chan.cnt += 16
        self._commit(chan, chan.cnt, reads, writes)

    def barrier(self):
        es = [self.pe, self.act, self.dve, self.pool]
        for e in es:
            for f in es:
                if f is e or f.cnt == 0:
                    continue
                if e.seen.get(f, 0) >= f.cnt:
                    continue
                e.seen[f] = f.cnt
                e.b.wait_ge(f.sem, f.cnt)

    def ps(self):
        i = self.ps_i
        self.ps_i = (i + 1) % 8
        return self.pst[i], self.psr[i]

    def wnext(self, src_ap, L, grp_res, ndma_parts=None):
        i = self.w_i
        self.w_i += 1
        k = i % NSLOT
        slot, res, ch = self.wslot[k], self.wres[k], self.wchan[k]
        need = self._need(self.sp, [grp_res], [res])
        self.sp_jobs.append((need, slot, src_ap, L, ch))
        ch.cnt += 16
        self._commit(ch, ch.cnt, [grp_res], [res])
        return slot, res

    def flush_sp(self):
        sp = self.sp
        for need, slot, src, L, ch in self.sp_jobs:
            for s, c in need:
                sp.b.wait_ge(s.sem, c)
            sp.b.dma_start(out=slot[:, 0:L], in_=src).then_inc(ch.sem, 16)
        self.sp_jobs = []

    def build(self):
        nc, NB, T, TC, NBC = self.nc, self.NB, self.T, self.TC, self.NBC
        es = self.es
        dt = nc.dram_tensor
        self.xT = dt("xT", [NB, 128, KC, T], F32, kind="ExternalInput").ap()
        self.cxT = dt("cxT", [NB, 128, KC, TC], F32, kind="ExternalInput").ap()
        self.cT = dt("cT", [128, KC, NBC], F32, kind="ExternalInput").ap()
        self.vecs_d = dt("vecs", [128, NVEC], F32, kind="ExternalInput").ap()
        self.consts_d = dt("consts", [128, NCONST], F32, kind="ExternalInput").ap()
        self.fupa_d = dt("fupa", [32, 2, 512], F32, kind="ExternalInput").ap()
        self.w_ada = dt("w_ada", [D, 9 * D], F32, kind="ExternalInput").ap()
        self.wi_d = [dt("ffn%d_wi" % (f + 1), [D, 2 * FH], F32, kind="ExternalInput").ap() for f in range(2)]
        self.wo_d = [dt("ffn%d_wo" % (f + 1), [FH, D], F32, kind="ExternalInput").ap() for f in range(2)]
        self.w_in = dt("w_in", [D, 7200], F32, kind="ExternalInput").ap()
        self.lru_w_d = [dt(n, [2, 8, 128, 128], F32, kind="ExternalInput").ap() for n in ("lru_wr", "lru_wi")]
        self.wp_d = [dt(n, [D, D], F32, kind="ExternalInput").ap() for n in ("w_out_gla", "w_out_lru", "w_o")]
        self.outT = dt("outT", [NB, 128, KC, T], F32, kind="ExternalOutput").ap()
        self.s_wi = [dt("s_wi%d" % f, [11, 128, 4096], BF16, kind="Internal").ap() for f in range(2)]
        self.s_wo = [dt("s_wo%d" % f, [8, 128, HC * 128], BF16, kind="Internal").ap() for f in range(2)]
        self.s_win = dt("s_win", [N_WIN_BLK, 128, 4096], BF16, kind="Internal").ap()
        self.s_wp = dt("s_wp", [6, 128, 4096], BF16, kind="Internal").ap()
        self.h1s = dt("h1s", [NB, 128, KC, T], F32, kind="Internal").ap()
        self.ofs = dt("ofs", [NB, 128, KC, T], F32, kind="Internal").ap()
        self.hfs = dt("hfs", [NB, 128, KC, T], F32, kind="Internal").ap()
        self.ch1s = dt("ch1s", [NB, 128, KC, TC], F32, kind="Internal").ap()
        self.h2s = dt("h2s", [NB, 128, KC, T], F32, kind="Internal").ap()
        self.deferred = []

        self.pe = Eng("pe", nc.tensor, self.sem())
        self.act = Eng("act", nc.scalar, self.sem())
        self.dve = Eng("dve", nc.vector, self.sem())
        self.pool = Eng("pool", nc.gpsimd, self.sem())
        self.sp = Eng("sp", nc.sync, None)
        pe, act, dve, pool = self.pe, self.act, self.dve, self.pool

        self.pst = [es.enter_context(nc.psum_tensor("ps%d" % i, [128, 512], F32)) for i in range(8)]
        self.psr = [Res() for _ in range(8)]
        self.ps_i = 0
        self.wslot = [self.sb("wslot%d" % i, [128, 4096], BF16) for i in range(NSLOT)]
        self.wres = [Res() for _ in range(NSLOT)]
        self.wchan = [self.chan() for _ in range(NSLOT)]
        self.w_i = 0

        sb = self.sb
        self.VECS = sb("vecs_sb", [128, NVEC], F32)
        self.CONSTS = sb("consts_sb", [128, NCONST], F32)
        self.FUPA = sb("fupa_sb", [32, 2, 512], BF16)
        self.ONES = sb("ones_sb", [128, 128], BF16)
        self.LRUW = sb("lruw_sb", [128, 32, 128], BF16)
        self.FDW = sb("fdw_sb", [128, KC, 32], BF16)
        self.MODT = sb("modt_sb", [128, 72, NBC], F32)
        self.AS = sb("as_sb", [128, 3, KC, NBC], F32)
        self.GS = sb("gs_sb", [128, 3, KC, NBC], F32)
        self.NEGC = sb("negc_sb", [128, 16], F32)
        self.NEGC2 = sb("negc2_sb", [128, 16], F32)
        self.HBIAS = sb("hbias_sb", [128, 32], F32)
        self.H = sb("h_sb", [128, KC, 512], F32)
        self.U = sb("u_sb", [128, KC, 512], BF16)
        self.FD = sb("fd_sb", [32, 512], BF16)
        self.S32 = [sb("s32_%d" % d, [128, 4, 256], F32) for d in range(2)]
        self.SBF = [sb("sbf_%d" % i, [128, 4, 256], BF16) for i in range(3)]
        self.CARRY = [sb("carry_%d" % d, [128, KC], F32) for d in range(2)]
        self.OB = [sb("ob_%d" % i, [128, KC, 128], F32) for i in range(1)]
        self.OFL = self.OB
        self.HSC = [sb("hsc_%d" % i, [128, 512], F32) for i in range(2)]
        self.HFLC = [sb("hflc_%d" % i, [128, 512], F32) for i in range(2)]
        R = Res
        self.rVECS, self.rCONSTS, self.rFUPA, self.rONES, self.rLRUW, self.rFDW = R(), R(), R(), R(), R(), R()
        self.rMODT, self.rAS, self.rGS, self.rNEGC = R(), R(), R(), R()
        self.rH, self.rU, self.rFD = R(), ResGroup(KC), R()
        self.rS32 = [ResGroup(4), ResGroup(4)]
        self.rSBF = [R(), R(), R()]
        self.rCARRY = [R(), R()]
        self.rM1, self.rHL = R(), R()
        self.rOB, self.rHSC, self.rHFLC = [R(), R()], [R(), R()], [R(), R()]
        self.rOFL = self.rOB
        self.chH, self.chHst, self.chH2 = self.chan(), self.chan(), self.chan()
        self.chOB = [self.chan(), self.chan()]
        self.chOFL = self.chOB
        self.chHSC = [self.chan(), self.chan()]
        self.chHFLC = [self.chan(), self.chan()]
        self.sbf_i = 0
        self.cnt2 = 0

        self.prologue()
        for b in range(NB):
            self.batch(b)
        for ch in [self.chH, self.chHst, self.chH2] + self.chOB + self.chHSC + self.chHFLC:
            if ch.cnt:
                pool.b.wait_ge(ch.sem, ch.cnt)
        self.flush_sp()
        self.es.close()
        return nc

    def vcol(self, name, i):
        o = VC[name] + i
        return self.VECS[:, o:o + 1]

    def prologue(self):
        nc, NBC = self.nc, self.NBC
        pe, act, dve, pool = self.pe, self.act, self.dve, self.pool
        op, dma = self.op, self.dma
        dma(pool, self.chan(), self.VECS[:, :], self.vecs_d[:, :], [], [self.rVECS])
        dma(pool, self.chan(), self.CONSTS[:, :], self.consts_d[:, :], [], [self.rCONSTS])
        dma(pool, self.chan(), self.FUPA[:, :, :], self.fupa_d[:, :, :], [], [self.rFUPA])
        chl = self.chan()
        for g in range(2):
            dma(pool, chl, self.LRUW[:, g * 16:(g + 1) * 16, :],
                self.lru_w_d[g].rearrange("d n c m -> c (d n) m"), [], [self.rLRUW])
        dma(pool, self.chan(), self.FDW[:, :, :],
            self.w_in.rearrange("(kc p) n -> p kc n", p=128)[:, :, 3072:3104], [], [self.rFDW])
        op(dve, lambda: nc.vector.memset(self.ONES[:, :], 1.0), [], [self.rONES])
        op(dve, lambda: nc.vector.memset(self.FD[:, :], 1.0), [], [self.rFD])
        self.g_wi = [Res(), Res()]
        self.g_wo = [Res(), Res()]
        self.g_win, self.g_wp = Res(), Res()

        def cast_wi(f):
            ch = self.chan()
            src = self.wi_d[f].rearrange("(kc p) n -> p kc n", p=128)
            for jb in range(11):
                dst = self.s_wi[f][jb].rearrange("p (g k n) -> p g k n", g=2, k=KC)
                for gu in range(2):
                    c0 = gu * FH + jb * 256
                    dma(pool, ch, dst[:, gu], src[:, :, c0:c0 + 256], [], [self.g_wi[f]])

        def cast_wo(f):
            ch = self.chan()
            src = self.wo_d[f].rearrange("(j p) n -> p j n", p=128)
            for c in range(8):
                dst = self.s_wo[f][c].rearrange("p (j m) -> p j m", j=HC)
                dma(pool, ch, dst, src[:, :, c * 128:(c + 1) * 128], [], [self.g_wo[f]])

        def cast_win():
            ch = self.chan()
            src = self.w_in.rearrange("(kc p) n -> p kc n", p=128)
            for n in WIN_ORDER:
                for j, c0 in enumerate(WIN_BLK[n]):
                    dst = self.s_win[WIN_IDX[(n, j)]].rearrange("p (k n) -> p k n", k=KC)
                    dma(pool, ch, dst, src[:, :, c0:c0 + 512], [], [self.g_win])

        def cast_wp():
            ch = self.chan()
            for w in range(3):
                src = self.wp_d[w].rearrange("(kc p) n -> p kc n", p=128)
                for ob in range(2):
                    dst = self.s_wp[w * 2 + ob].rearrange("p (k n) -> p k n", k=KC)
                    dma(pool, ch, dst, src[:, :, ob * 512:(ob + 1) * 512], [], [self.g_wp])

        cast_wi(0)
        cast_wo(0)
        with ExitStack() as st:
            CTs = self.sb("ct_sb", [128, KC, NBC], F32, st)
            SCs = self.sb("sc_sb", [128, KC, NBC], BF16, st)
            WA = [self.sb("wa_%d" % i, [128, KC, 512], BF16, st) for i in range(2)]
            rCT, rSC, rWA = Res(), Res(), [Res(), Res()]
            chw = [self.chan(), self.chan()]
            dma(pool, self.chan(), CTs[:, :, :], self.cT[:, :, :], [], [rCT])
            op(act, lambda: nc.scalar.activation(out=SCs[:, :, :], in_=CTs[:, :, :], func=AF.Silu), [rCT], [rSC])
            wsrc = self.w_ada.rearrange("(kc p) n -> p kc n", p=128)
            for blk in range(18):
                k = blk % 2
                dma(pool, chw[k], WA[k][:, :, :], wsrc[:, :, blk * 512:(blk + 1) * 512], [], [rWA[k]])
                if blk == 8:
                    cast_win()
                pt, pr = self.ps()
                for m4 in range(4):
                    self.mmacc(pr, pt[:, m4 * 8:m4 * 8 + NBC],
                               [(WA[k][:, kc, m4 * 128:(m4 + 1) * 128], SCs[:, kc, :]) for kc in range(KC)],
                               [rWA[k], rSC])
                for m4 in range(4):
                    m = blk * 4 + m4
                    op(dve, lambda m=m, m4=m4: nc.vector.tensor_scalar(
                        out=self.MODT[:, m, :], in0=pt[:, m4 * 8:m4 * 8 + NBC], scalar1=self.vcol("bada", m),
                        scalar2=None, op0=ALU.add), [pr, self.rVECS], [self.rMODT])
            M4 = self.MODT[:, :, :].rearrange("p (j c) b -> p j c b", j=9)
            for s in range(3):
                for b in range(NBC):
                    op(dve, lambda s=s, b=b: nc.vector.scalar_tensor_tensor(
                        out=self.AS[:, s, :, b], in0=M4[:, 3 * s + 1, :, b], scalar=1.0,
                        in1=self.VECS[:, VC["nw"] + s * 8:VC["nw"] + s * 8 + 8], op0=ALU.add, op1=ALU.mult),
                       [self.rMODT, self.rVECS], [self.rAS])
                gf = 0.5
                op(dve, lambda s=s, gf=gf: nc.vector.tensor_scalar(
                    out=self.GS[:, s, :, :], in0=M4[:, 3 * s + 2, :, :], scalar1=gf, scalar2=None, op0=ALU.mult),
                   [self.rMODT], [self.rGS])
            lam = self.VECS[:, VC["lam"]:VC["lam"] + 16]
            op(act, lambda: nc.scalar.activation(out=self.NEGC[:, :], in_=lam, func=AF.Exp, scale=-1.0),
               [self.rVECS], [self.rNEGC])
            op(act, lambda: nc.scalar.activation(out=self.NEGC[:, :], in_=self.NEGC[:, :], func=AF.Ln, bias=1.0),
               [self.rNEGC], [self.rNEGC])
            op(dve, lambda: nc.vector.tensor_scalar(out=self.NEGC2[:, :], in0=self.NEGC[:, :], scalar1=-4.0,
                                                    scalar2=None, op0=ALU.mult), [self.rNEGC], [self.rNEGC])
            op(dve, lambda: nc.vector.tensor_scalar(out=self.NEGC[:, :], in0=self.NEGC[:, :], scalar1=-8.0,
                                                    scalar2=None, op0=ALU.mult), [self.rNEGC], [self.rNEGC])
            op(dve, lambda: nc.vector.tensor_scalar(out=self.HBIAS[:, :], in0=self.VECS[:, VC["lbr"]:VC["lbr"] + 32],
                                                    scalar1=0.5, scalar2=None, op0=ALU.mult),
               [self.rVECS], [self.rNEGC])
            self.barrier()
        cast_wp()
        cast_wi(1)
        cast_wo(1)

    def norm_stats(self, st, TWc, nchunk, src, rsrc, inv_n, tag):
        nc = self.nc
        SQ = self.sb("sq_" + tag, [128, KC, 512], BF16, st)
        RS = self.sb("rs_" + tag, [128, 512], F32, st)
        rSQa, rSQb, rRS = Res(), Res(), Res()
        H = self.H
        self.op(self.act, lambda: nc.scalar.activation(out=SQ[:, 0:4, :TWc], in_=H[:, 0:4, :TWc], func=AF.Square),
                [self.rH], [rSQa])
        self.op(self.dve, lambda: nc.vector.tensor_tensor(out=SQ[:, 4:8, :TWc], in0=H[:, 4:8, :TWc],
                                                          in1=H[:, 4:8, :TWc], op=ALU.mult), [self.rH], [rSQb])
        pt, pr = self.ps()
        self.mmacc(pr, pt[:, :TWc], [(self.ONES[:, :], SQ[:, c, :TWc]) for c in range(KC)],
                   [self.rONES, rSQa, rSQb])
        self.op(self.act, lambda: nc.scalar.activation(out=RS[:, :TWc], in_=pt[:, :TWc], func=AF.Ln,
                                                       bias=self.EPSB[:, 0:1], scale=inv_n), [pr, self.rEPSB], [rRS])
        self.op(self.act, lambda: nc.scalar.activation(out=RS[:, :TWc], in_=RS[:, :TWc], func=AF.Exp, scale=-0.5),
                [rRS], [rRS])
        return RS, rRS

    def norm_mod(self, s, b, TWc):
        nc = self.nc
        with ExitStack() as st:
            RS, rRS = self.norm_stats(st, TWc, KC, None, None, 1.0 / D, "nm")
            T = self.sb("nm_t", [128, KC, 512], F32, st)
            rT = [Res(), Res()]
            RSb = RS[:, :TWc].unsqueeze(1).broadcast_to([128, 4, TWc])
            for hf in range(2):
                self.op(self.dve, lambda hf=hf: nc.vector.tensor_tensor(
                    out=T[:, hf * 4:hf * 4 + 4, :TWc], in0=self.H[:, hf * 4:hf * 4 + 4, :TWc], in1=RSb, op=ALU.mult),
                    [self.rH, rRS], [rT[hf]])
            for c in range(4):
                self.op(self.act, lambda c=c: nc.scalar.activation(
                    out=self.U[:, c, :TWc], in_=T[:, c, :TWc], func=AF.Identity,
                    bias=self.MODT[:, (3 * s) * 8 + c, b:b + 1], scale=self.AS[:, s, c, b:b + 1]),
                    [rT[0], self.rMODT, self.rAS], [self.rU.parts[c]])
            for c in range(4, 8):
                self.op(self.dve, lambda c=c: nc.vector.tensor_scalar(
                    out=self.U[:, c, :TWc], in0=T[:, c, :TWc], scalar1=self.AS[:, s, c, b:b + 1],
                    scalar2=self.MODT[:, (3 * s) * 8 + c, b:b + 1], op0=ALU.mult, op1=ALU.add),
                    [rT[1], self.rMODT, self.rAS], [self.rU.parts[c]])
            self.barrier()

    def ffn(self, f, s, b, TWc):
        nc = self.nc
        pe, act, dve = self.pe, self.act, self.dve
        self.norm_mod(s, b, TWc)
        with ExitStack() as st:
            ACTH = self.sb("acth", [128, HC, 512], BF16, st)
            SG = [self.sb("sg%d" % i, [128, 512], F32, st) for i in range(2)]
            rACTH, rSG = Res(), [Res(), Res()]
            for jb in range(11):
                slot, wr = self.wnext(self.s_wi[f][jb], 4096, self.g_wi[f])
                wv = slot[:, :].rearrange("p (g k n) -> p g k n", g=2, k=KC)
                for jj in range(2):
                    j = jb * 2 + jj
                    pg, rg = self.ps()
                    self.mmacc(rg, pg[:, :TWc], [(wv[:, 0, kc, jj * 128:(jj + 1) * 128], self.U[:, kc, :TWc])
                                                 for kc in range(KC)], [wr, self.rU])
                    pu, ru = self.ps()
                    self.mmacc(ru, pu[:, :TWc], [(wv[:, 1, kc, jj * 128:(jj + 1) * 128], self.U[:, kc, :TWc])
                                                 for kc in range(KC)], [wr, self.rU])
                    k = j % 2
                    self.op(act, lambda k=k, pg=pg: nc.scalar.activation(out=SG[k][:, :TWc], in_=pg[:, :TWc],
                                                                         func=AF.Silu), [rg], [rSG[k]])
                    self.op(dve, lambda k=k, pu=pu, j=j: nc.vector.tensor_tensor(
                        out=ACTH[:, j, :TWc], in0=SG[k][:, :TWc], in1=pu[:, :TWc], op=ALU.mult),
                        [rSG[k], ru], [rACTH])
            for c in range(KC):
                slot, wr = self.wnext(self.s_wo[f][c], HC * 128, self.g_wo[f])
                wv = slot[:, 0:HC * 128].rearrange("p (j m) -> p j m", j=HC)
                po, ro = self.ps()
                self.mmacc(ro, po[:, :TWc], [(wv[:, j, :], ACTH[:, j, :TWc]) for j in range(HC)], [wr, rACTH])
                self.op(dve, lambda c=c, po=po: nc.vector.scalar_tensor_tensor(
                    out=self.H[:, c, :TWc], in0=po[:, :TWc], scalar=self.GS[:, s, c, b:b + 1],
                    in1=self.H[:, c, :TWc], op0=ALU.mult, op1=ALU.add), [ro, self.rGS, self.rH], [self.rH])
            self.barrier()

    def norm_mod_p(self, s, b, TWc, P, Uout, rUout):
        nc = self.nc
        H, SQ, RS, TMP = self.H, P["ACTH"], P["RS"], P["TMP"]
        rSQ, rRS, rTMP = P["rACTH"], P["rRS"], P["rTMP"]
        self.op(self.act, lambda: nc.scalar.activation(out=SQ[:, 0:4, :TWc], in_=H[:, 0:4, :TWc], func=AF.Square),
                [self.rH], [rSQ])
        self.op(self.dve, lambda: nc.vector.tensor_tensor(out=SQ[:, 4:8, :TWc], in0=H[:, 4:8, :TWc],
                                                          in1=H[:, 4:8, :TWc], op=ALU.mult), [self.rH], [rSQ])
        pt, pr = self.ps()
        self.mmacc(pr, pt[:, :TWc], [(self.ONES[:, :], SQ[:, c, :TWc]) for c in range(KC)], [self.rONES, rSQ])
        self.op(self.act, lambda: nc.scalar.activation(out=RS[:, :TWc], in_=pt[:, :TWc], func=AF.Ln,
                                                       bias=self.EPSB[:, 0:1], scale=1.0 / D), [pr, self.rEPSB], [rRS])
        self.op(self.act, lambda: nc.scalar.activation(out=RS[:, :TWc], in_=RS[:, :TWc], func=AF.Exp, scale=-0.5),
                [rRS], [rRS])
        for c in range(KC):
            k = c % 2
            self.op(self.dve, lambda: nc.vector.tensor_tensor(
                out=TMP[k][:, :TWc], in0=H[:, c, :TWc], in1=RS[:, :TWc], op=ALU.mult), [self.rH, rRS], [rTMP[k]])
            if c % 2 == 0:
                self.op(self.act, lambda: nc.scalar.activation(
                    out=Uout[:, c, :TWc], in_=TMP[k][:, :TWc], func=AF.Identity,
                    bias=self.MODT[:, (3 * s) * 8 + c, b:b + 1], scale=self.AS[:, s, c, b:b + 1]),
                    [rTMP[k], self.rMODT, self.rAS], [rUout.parts[c]])
            else:
                self.op(self.dve, lambda: nc.vector.tensor_scalar(
                    out=Uout[:, c, :TWc], in0=TMP[k][:, :TWc], scalar1=self.AS[:, s, c, b:b + 1],
                    scalar2=self.MODT[:, (3 * s) * 8 + c, b:b + 1], op0=ALU.mult, op1=ALU.add),
                    [rTMP[k], self.rMODT, self.rAS], [rUout.parts[c]])

    def ffn_gen(self, f, s, b, TWc, P):
        nc = self.nc
        act, dve = self.act, self.dve
        UF, rUF, ACTH, rACTH, SG, rSG = P["UF"], P["rUF"], P["ACTH"], P["rACTH"], P["SG"], P["rSG"]
        self.norm_mod_p(s, b, TWc, P, UF, rUF)
        yield
        for jb in range(11):
            slot, wr = self.wnext(self.s_wi[f][jb], 4096, self.g_wi[f])
            wv = slot[:, :].rearrange("p (g k n) -> p g k n", g=2, k=KC)
            for jj in range(2):
                j = jb * 2 + jj
                pg, rg = self.ps()
                self.mmacc(rg, pg[:, :TWc], [(wv[:, 0, kc, jj * 128:(jj + 1) * 128], UF[:, kc, :TWc])
                                             for kc in range(KC)], [wr, rUF])
                pu, ru = self.ps()
                self.mmacc(ru, pu[:, :TWc], [(wv[:, 1, kc, jj * 128:(jj + 1) * 128], UF[:, kc, :TWc])
                                             for kc in range(KC)], [wr, rUF])
                k = j % 2
                self.op(act, lambda: nc.scalar.activation(out=SG[k][:, :TWc], in_=pg[:, :TWc], func=AF.Silu),
                        [rg], [rSG[k]])
                self.op(dve, lambda: nc.vector.tensor_tensor(
                    out=ACTH[:, j, :TWc], in0=SG[k][:, :TWc], in1=pu[:, :TWc], op=ALU.mult),
                    [rSG[k], ru], [rACTH])
            yield
        for c in range(KC):
            slot, wr = self.wnext(self.s_wo[f][c], HC * 128, self.g_wo[f])
            wv = slot[:, 0:HC * 128].rearrange("p (j m) -> p j m", j=HC)
            po, ro = self.ps()
            self.mmacc(ro, po[:, :TWc], [(wv[:, j, :], ACTH[:, j, :TWc]) for j in range(HC)], [wr, rACTH])
            self.op(dve, lambda: nc.vector.scalar_tensor_tensor(
                out=self.H[:, c, :TWc], in0=po[:, :TWc], scalar=self.GS[:, s, c, b:b + 1],
                in1=self.H[:, c, :TWc], op0=ALU.mult, op1=ALU.add), [ro, self.rGS, self.rH], [self.rH])
            yield

    def final_norm_p(self, TWc, P):
        nc = self.nc
        H, SQ, RS = self.H, P["ACTH"], P["RS"]
        rSQ, rRS = P["rACTH"], P["rRS"]
        self.op(self.act, lambda: nc.scalar.activation(out=SQ[:, 0:4, :TWc], in_=H[:, 0:4, :TWc], func=AF.Square),
                [self.rH], [rSQ])
        self.op(self.dve, lambda: nc.vector.tensor_tensor(out=SQ[:, 4:8, :TWc], in0=H[:, 4:8, :TWc],
                                                          in1=H[:, 4:8, :TWc], op=ALU.mult), [self.rH], [rSQ])
        pt, pr = self.ps()
        self.mmacc(pr, pt[:, :TWc], [(self.ONES[:, :], SQ[:, c, :TWc]) for c in range(KC)], [self.rONES, rSQ])
        self.op(self.act, lambda: nc.scalar.activation(out=RS[:, :TWc], in_=pt[:, :TWc], func=AF.Ln,
                                                       bias=self.EPSB[:, 0:1], scale=1.0 / D), [pr, self.rEPSB], [rRS])
        self.op(self.act, lambda: nc.scalar.activation(out=RS[:, :TWc], in_=RS[:, :TWc], func=AF.Exp, scale=-0.5),
                [rRS], [rRS])
        for c in range(KC):
            self.op(self.dve, lambda: nc.vector.scalar_tensor_tensor(
                out=H[:, c, :TWc], in0=H[:, c, :TWc], scalar=self.vcol("fnw", c), in1=RS[:, :TWc],
                op0=ALU.mult, op1=ALU.mult), [self.rH, rRS, self.rVECS], [self.rH])

    def ffn2_task(self, b, t0, P):
        TW = self.TW
        self.dma(self.pool, self.chH, self.H[:, :, :TW], self.h2s[b][:, :, t0:t0 + TW],
                 [self.dr(("h2s", b, t0))], [self.rH])
        yield from self.ffn_gen(1, 2, b, TW, P)
        self.final_norm_p(TW, P)
        self.dma(self.pool, self.chHst, self.outT[b][:, :, t0:t0 + TW], self.H[:, :, :TW], [self.rH],
                 [self.dr(("out", b, t0))])
        yield

    def pass1(self, b, tiles, M0, prev):
        TW = self.TW
        with ExitStack() as stp:
            P = {"UF": self.sb("p_uf", [128, KC, 512], BF16, stp),
                 "ACTH": self.sb("p_acth", [128, HC, 512], BF16, stp),
                 "SG": [self.sb("p_sg%d" % i, [128, 512], F32, stp) for i in range(2)],
                 "TMP": [self.sb("p_tmp%d" % i, [128, 512], F32, stp) for i in range(2)],
                 "RS": self.sb("p_rs", [128, 512], F32, stp),
                 "rUF": ResGroup(KC), "rACTH": Res(), "rSG": [Res(), Res()], "rTMP": [Res(), Res()], "rRS": Res()}

            def f_head(t0):
                self.dma(self.pool, self.chH, self.H[:, :, :TW], self.xT[b][:, :, t0:t0 + TW], [], [self.rH])
                yield from self.ffn_gen(0, 0, b, TW, P)

            pend = list(prev)

            def ctx_ffn(bn):
                TC = self.TC
                self.dma(self.pool, self.chH, self.H[:, :, :TC], self.cxT[bn], [], [self.rH])
                yield from self.ffn_gen(0, 0, self.NB, TC, P)
                self.dma(self.pool, self.chHst, self.ch1s[bn], self.H[:, :, :TC], [self.rH],
                         [self.dr(("ch1s", bn))])
                yield

            def f_tail(t0):
                self.dma(self.pool, self.chHst, self.h1s[b][:, :, t0:t0 + TW], self.H[:, :, :TW], [self.rH],
                         [self.dr(("h1s", b, t0))])
                self.norm_mod(1, b, TW)

            for i in range(-1, len(tiles)):
                if i < 0:
                    M = M0
                else:
                    t0 = tiles[i]
                    M = self.gla(b, 0, TW, "p1", t0, coro=self.lru(b, 0, TW, "p1", t0, 64))
                def f_stream(i=i):
                    if pend and (i == -1 or i == len(tiles) - 1):
                        pb, pt0 = pend.pop(0)
                        yield from self.ffn2_task(pb, pt0, P)
                    if i + 1 < len(tiles):
                        yield from f_head(tiles[i + 1])
                F = f_stream()
                liveM = liveF = True
                while liveM or liveF:
                    for _ in range(2):
                        if liveM:
                            try:
                                next(M)
                            except StopIteration:
                                liveM = False
                    if liveF:
                        try:
                            next(F)
                        except StopIteration:
                            liveF = False
                if i + 1 < len(tiles):
                    f_tail(tiles[i + 1])
            while pend:
                pb, pt0 = pend.pop(0)
                for _ in self.ffn2_task(pb, pt0, P):
                    pass
            self.barrier()

    def win_block(self, name, j):
        slot, wr = self.wnext(self.s_win[WIN_IDX[(name, j)]], 4096, self.g_win)
        return slot[:, :].rearrange("p (k n) -> p k n", k=KC), wr

    def wp_block(self, w, ob):
        slot, wr = self.wnext(self.s_wp[w * 2 + ob], 4096, self.g_wp)
        return slot[:, :].rearrange("p (k n) -> p k n", k=KC), wr

    def proj_fm(self, wv, wr, cc, TWc):
        pt, pr = self.ps()
        self.mmacc(pr, pt[:, :TWc], [(wv[:, kc, cc * 128:(cc + 1) * 128], self.U[:, kc, :TWc]) for kc in range(KC)],
                   [wr, self.rU])
        return pt, pr

    def proj_tm(self, wv, wr, sub):
        pt, pr = self.ps()
        self.mmacc(pr, pt[:, :], [(self.U[:, kc, sub * 128:(sub + 1) * 128], wv[:, kc, :]) for kc in range(KC)],
                   [wr, self.rU])
        return pt, pr

    def gla(self, b, d, TWc, mode, t0, coro=None):
        nc = self.nc
        pe, act, dve, pool = self.pe, self.act, self.dve, self.pool
        op = self.op
        need_o = mode != "ctx"
        ns = TWc // 128
        CS = self.CONSTS
        TRI = CS[:, CC["tri%d" % d]:CC["tri%d" % d] + 128]
        UTRI = CS[:, CC["utri%d" % d]:CC["utri%d" % d] + 128]
        MASK = CS[:, CC["mask%d" % d]:CC["mask%d" % d] + 128].unsqueeze(1).broadcast_to([128, 4, 128])
        with ExitStack() as st:
            st_a, st_b = ExitStack(), ExitStack()
            sbo = lambda n, sh, dt_: self.sb("g_" + n, sh, dt_, st)
            sba = lambda n, sh, dt_: self.sb("g_" + n, sh, dt_, st_a)
            sb = lambda n, sh, dt_: self.sb("g_" + n, sh, dt_, st_b)
            O = sbo("o", [128, KC, 512], F32) if mode == "p2" else None
            rO = Res()
            V = sba("v", [128, 4, 1024], BF16)
            KEND = sba("kend", [128, 4, 512], BF16)
            QDEC = sba("qdec", [128, 4, 512], BF16) if need_o else None
            KINV = sba("kinv", [128, 4, 512], BF16) if need_o else None
            DEC = sba("dec", [128, 4, 2, 4], F32)
            ST = [sba("st%d" % i, [128, 512], BF16) for i in range(2)]
            rV, rKEND, rQDEC, rKINV, rDEC = Res(), Res(), Res(), Res(), Res()
            QT = sb("qt", [128, 4, 512], F32) if need_o else None
            KT = sb("kt", [128, 4, 512], F32) if need_o else None
            rQT, rKT = Res(), Res()
            EX = sb("ex", [128, 512], F32)
            LB = [sb("lb%d" % i, [128, 512], F32) for i in range(2)]
            EEND = [sb("eend%d" % i, [128, 512], F32) for i in range(2)]
            EB = [sb("eb%d" % i, [128, 512], F32) for i in range(2)]
            EINV = [sb("einv%d" % i, [128, 512], F32) for i in range(2)]
            rEX, rLB, rEEND, rEB, rEINV, rST = Res(), [Res(), Res()], [Res(), Res()], [Res(), Res()], \
                [Res(), Res()], [Res(), Res()]

            order = list(range(ns)) if d == 0 else list(range(ns - 1, -1, -1))
            wkh = {}

            def proj_fd():
                pt, pr = self.ps()
                self.mmacc(pr, pt[0:16, :TWc], [(self.FDW[:, kc, d * 16:(d + 1) * 16], self.U[:, kc, :TWc])
                                                for kc in range(KC)], [self.rFDW, self.rU])
                op(act, lambda: nc.scalar.copy(out=self.FD[0:16, :TWc], in_=pt[0:16, :TWc]), [pr], [self.rFD])

            def proj_q():
                if not need_o:
                    return
                wq, rq = self.win_block("q", 0)
                for h in range(4):
                    pq, prq = self.proj_fm(wq, rq, h, TWc)
                    op(act, lambda h=h, pq=pq: nc.scalar.copy(out=QT[:, h, :TWc], in_=pq[:, :TWc]), [prq], [rQT])

            def proj_k():
                wkh["w"], wkh["r"] = self.win_block("k", 0)
                if need_o:
                    for h in range(4):
                        pk, prk = self.proj_fm(wkh["w"], wkh["r"], h, TWc)
                        op(dve, lambda h=h, pk=pk: nc.vector.tensor_copy(out=KT[:, h, :TWc], in_=pk[:, :TWc]),
                           [prk], [rKT])

            def proj_v(half):
                wv_, rv_ = self.win_block("v", half)
                for sub in range(ns):
                    pv, prv = self.proj_tm(wv_, rv_, sub)
                    if half == 0:
                        op(dve, lambda pv=pv, sub=sub: nc.vector.tensor_copy(out=V[:, sub, 0:512], in_=pv[:, :]),
                           [prv], [rV])
                    else:
                        op(act, lambda pv=pv, sub=sub: nc.scalar.copy(out=V[:, sub, 512:1024], in_=pv[:, :]),
                           [prv], [rV])

            pbs = {}

            def A1(pi):
                sub, k2 = order[pi], pi % 2
                cols = slice(sub * 128, (sub + 1) * 128)
                pz, rz = self.ps()
                self.mmacc(rz, pz[:, :], [(self.FD[0:17, cols], self.FUPA[0:17, d, :])], [self.rFD, self.rFUPA])
                op(act, lambda: nc.scalar.activation(out=EX[:, :], in_=pz[:, :], func=AF.Exp, scale=-1.0),
                   [rz], [rEX])
                op(act, lambda: nc.scalar.activation(out=LB[k2][:, :], in_=EX[:, :], func=AF.Ln, bias=1.0),
                   [rEX], [rLB[k2]])

            def A2(pi):
                sub, k2 = order[pi], pi % 2
                prx, rrx = self.ps()
                self.mmacc(rrx, prx[:, :], [(UTRI, LB[k2][:, :])], [self.rCONSTS, rLB[k2]])
                op(act, lambda: nc.scalar.activation(out=EEND[k2][:, :], in_=prx[:, :], func=AF.Exp),
                   [rrx], [rEEND[k2]])
                pb, rb = self.ps()
                self.mm(rb, [(pb[:, h * 128:(h + 1) * 128], LB[k2][:, h * 128:(h + 1) * 128], TRI, True)
                             for h in range(4)], [self.rCONSTS, rLB[k2]])
                op(act, lambda: nc.scalar.activation(out=EB[k2][:, :], in_=pb[:, :], func=AF.Exp),
                   [rb], [rEB[k2]])
                EB3 = EB[k2][:, :].rearrange("p (h t) -> p h t", h=4)
                for ch in range(2):
                    col = ch * 64 + (63 if d == 0 else 0)
                    op(dve, lambda: nc.vector.tensor_copy(out=DEC[:, sub, ch, :], in_=EB3[:, :, col]),
                       [rEB[k2]], [rDEC])
                if need_o:
                    op(act, lambda: nc.scalar.activation(out=EINV[k2][:, :], in_=pb[:, :], func=AF.Exp,
                                                         scale=-1.0), [rb], [rEINV[k2]])

            def B(pi):
                sub, k2 = order[pi], pi % 2
                cols = slice(sub * 128, (sub + 1) * 128)
                pk, prk = self.proj_tm(wkh["w"], wkh["r"], sub)
                op(dve, lambda: nc.vector.tensor_tensor(
                    out=KEND[:, sub, :], in0=pk[:, :], in1=EEND[k2][:, :], op=ALU.mult), [prk, rEEND[k2]], [rKEND])
                if need_o:
                    EB3 = EB[k2][:, :].rearrange("p (h t) -> p h t", h=4)
                    op(dve, lambda: nc.vector.scalar_tensor_tensor(
                        out=QDEC[:, :, cols], in0=QT[:, :, cols], scalar=float(128 ** -0.5), in1=EB3,
                        op0=ALU.mult, op1=ALU.mult), [rQT, rEB[k2]], [rQDEC])
                    EI3 = EINV[k2][:, :].rearrange("p (h t) -> p h t", h=4)
                    op(dve, lambda: nc.vector.tensor_tensor(
                        out=KINV[:, :, cols], in0=KT[:, :, cols], in1=EI3, op=ALU.mult), [rKT, rEINV[k2]], [rKINV])

            proj_fd()
            A1(0)
            A1(1)
            proj_q()
            A2(0)
            A2(1)
            proj_k()
            B(0)
            B(1)
            if ns == 4:
                A1(2)
                A1(3)
                proj_v(0)
                A2(2)
                A2(3)
                proj_v(1)
                B(2)
                B(3)
            else:
                proj_v(0)
                proj_v(1)
            S32, rS32 = self.S32[d], self.rS32[d]
            co = [0, 1] if d == 0 else [1, 0]
            if need_o:
                nxt = (self.sbf_i + 1) % 3
                self.sbf_i = nxt
                op(act, lambda nxt=nxt: nc.scalar.copy(out=self.SBF[nxt][:, :, :], in_=S32[:, :, :]),
                   [rS32], [self.rSBF[nxt]])
            def scan(si, sub):
                k2 = si % 2
                cols = slice(sub * 128, (sub + 1) * 128)
                if need_o:
                    pss, rss = self.ps()
                    self.mm(rss, [(pss[:, h * 128:(h + 1) * 128], KINV[:, h, cols], QDEC[:, h, cols], True)
                                  for h in range(4)], [rKINV, rQDEC])
                    op(dve, lambda k2=k2, pss=pss: nc.vector.tensor_tensor(
                        out=ST[k2][:, :].rearrange("p (h t) -> p h t", h=4),
                        in0=pss[:, :].rearrange("p (h t) -> p h t", h=4), in1=MASK, op=ALU.mult),
                       [rss, self.rCONSTS], [rST[k2]])
                    sA = self.sbf_i
                yield
                sbufs = [None, None]
                if need_o:
                    sbufs[0] = sA
                for ci, ch in enumerate(co):
                    rows = slice(ch * 64, (ch + 1) * 64)
                    pk0, rk0 = self.ps()
                    pk1, rk1 = self.ps()
                    pks, rks = [pk0, pk1], [rk0, rk1]
                    for hh in range(2):
                        self.mm(rks[hh], [(pks[hh][:, q * 256:(q + 1) * 256],
                                           KEND[rows, sub, (hh * 2 + q) * 128:(hh * 2 + q + 1) * 128],
                                           V[rows, sub, (hh * 2 + q) * 256:(hh * 2 + q + 1) * 256], True)
                                          for q in range(2)], [rKEND, rV])
                    for h in range(4):
                        hh, q = h // 2, h % 2
                        op(dve, lambda h=h, hh=hh, q=q, sub=sub, ch=ch, pks=pks: nc.vector.scalar_tensor_tensor(
                            out=S32[:, h, :], in0=S32[:, h, :], scalar=DEC[:, sub, ch, h:h + 1],
                            in1=pks[hh][:, q * 256:(q + 1) * 256], op0=ALU.mult, op1=ALU.add),
                            [rS32.parts[h], rDEC, rks[hh]], [rS32.parts[h]])
                    if need_o:
                        nxt = (self.sbf_i + 1) % 3
                        self.sbf_i = nxt
                        op(act, lambda nxt=nxt: nc.scalar.copy(out=self.SBF[nxt][:, :, :], in_=S32[:, :, :]),
                           [rS32], [self.rSBF[nxt]])
                        if ci == 0:
                            sbufs[1] = nxt
                    yield
                if not need_o:
                    return
                banks = [self.ps(), self.ps()]
                for g in range(8):
                    h, ec = g // 2, g % 2
                    pt_, pr_ = banks[g // 4]
                    oc = (g % 4) * 128
                    items = [(pt_[:, oc:oc + 128], V[:, sub, h * 256 + ec * 128:h * 256 + (ec + 1) * 128],
                              ST[k2][:, h * 128:(h + 1) * 128], True)]
                    for ci, ch in enumerate(co):
                        sbi = sbufs[ci]
                        items.append((pt_[:, oc + ch * 64:oc + (ch + 1) * 64],
                                      self.SBF[sbi][:, h, ec * 128:(ec + 1) * 128],
                                      QDEC[:, h, sub * 128 + ch * 64:sub * 128 + (ch + 1) * 64], False))
                    self.mm(pr_, items, [rV, rST[k2], self.rSBF[sbufs[0]], self.rSBF[sbufs[1]], rQDEC])
                tcols = slice(t0 + sub * 128, t0 + (sub + 1) * 128)
                if mode == "p1":
                    kb = 0
                    for gq in range(2):
                        pt_, pr_ = banks[gq]
                        src3 = pt_[:, :].rearrange("p (g t) -> p g t", g=4)
                        op(act, lambda kb=kb, gq=gq, src3=src3: nc.scalar.copy(
                            out=self.OB[kb][:, gq * 4:(gq + 1) * 4, :], in_=src3), [pr_], [self.rOB[kb]])
                    self.dma(pool, self.chOB[kb], self.ofs[b][:, :, tcols], self.OB[kb][:, :, :],
                             [self.rOB[kb]], [self.dr(("ofs", b, t0, sub))])
                else:
                    kb = 0
                    self.dma(pool, self.chOFL[kb], self.OFL[kb][:, :, :], self.ofs[b][:, :, tcols],
                             [self.dr(("ofs", b, t0, sub))], [self.rOFL[kb]])
                    for gq in range(2):
                        pt_, pr_ = banks[gq]
                        src3 = pt_[:, :].rearrange("p (g t) -> p g t", g=4)
                        op(dve, lambda kb=kb, gq=gq, src3=src3, cols=cols: nc.vector.tensor_tensor(
                            out=O[:, gq * 4:(gq + 1) * 4, cols], in0=src3, in1=self.OFL[kb][:, gq * 4:(gq + 1) * 4, :],
                            op=ALU.add), [pr_, self.rOFL[kb]], [rO])
                yield

            self.barrier()
            st_b.close()

            def scan_all():
                for si, sub in enumerate(order):
                    yield from scan(si, sub)
            g1, g2 = scan_all(), (coro if coro is not None else iter(()))
            live1 = live2 = True
            while live1 or live2:
                if live1:
                    try:
                        next(g1)
                    except StopIteration:
                        live1 = False
                    yield
                for _ in range(2):
                    if live2:
                        try:
                            next(g2)
                        except StopIteration:
                            live2 = False
                        yield
            self.barrier()
            st_a.close()
            if mode == "p2":
                self.gla_post(b, TWc, O, rO, st)
            self.barrier()

    def gla_post(self, b, TWc, O, rO, st):
        nc = self.nc
        act, dve = self.act, self.dve
        op = self.op
        SQ2 = self.sb("gp_sq", [128, KC, 512], BF16, st)
        RS = [self.sb("gp_rs%d" % h, [128, 512], F32, st) for h in range(4)]
        rSQ2, rRS = Res(), [Res() for _ in range(4)]
        OG = self.sb("gp_og", [128, KC, 512], BF16, st)
        rOG = Res()
        T1 = [self.sb("gp_t1%d" % i, [128, 512], F32, st) for i in range(2)]
        SGT = [self.sb("gp_sg%d" % i, [128, 512], F32, st) for i in range(2)]
        rT1, rSGT = [Res(), Res()], [Res(), Res()]
        op(act, lambda: nc.scalar.activation(out=SQ2[:, :, :TWc], in_=O[:, :, :TWc], func=AF.Square), [rO], [rSQ2])
        for h in range(4):
            pt, pr = self.ps()
            self.mmacc(pr, pt[:, :TWc], [(self.ONES[:, :], SQ2[:, 2 * h + e, :TWc]) for e in range(2)],
                       [self.rONES, rSQ2])
            op(act, lambda h=h, pt=pt: nc.scalar.activation(out=RS[h][:, :TWc], in_=pt[:, :TWc], func=AF.Ln,
                                                            bias=self.EPSB[:, 0:1], scale=1.0 / 256),
               [pr, self.rEPSB], [rRS[h]])
            op(act, lambda h=h: nc.scalar.activation(out=RS[h][:, :TWc], in_=RS[h][:, :TWc], func=AF.Exp, scale=-0.5),
               [rRS[h]], [rRS[h]])
        for gb in range(2):
            wg, rg = self.win_block("g", gb)
            for cc in range(4):
                c = gb * 4 + cc
                h, ec, k = c // 2, c % 2, c % 2
                pg, prg = self.proj_fm(wg, rg, cc, TWc)
                op(act, lambda k=k, pg=pg: nc.scalar.activation(out=SGT[k][:, :TWc], in_=pg[:, :TWc], func=AF.Silu),
                   [prg], [rSGT[k]])
                op(dve, lambda k=k, c=c, h=h: nc.vector.tensor_tensor(
                    out=T1[k][:, :TWc], in0=O[:, c, :TWc], in1=RS[h][:, :TWc], op=ALU.mult), [rO, rRS[h]], [rT1[k]])
                op(dve, lambda k=k, c=c, ec=ec: nc.vector.scalar_tensor_tensor(
                    out=OG[:, c, :TWc], in0=T1[k][:, :TWc], scalar=self.vcol("gnw", ec), in1=SGT[k][:, :TWc],
                    op0=ALU.mult, op1=ALU.mult), [rT1[k], rSGT[k], self.rVECS], [rOG])
        for ob in range(2):
            w, rw = self.wp_block(0, ob)
            wga, rga = self.win_block("ga", ob)
            for cc in range(4):
                c = ob * 4 + cc
                k = c % 2
                py, pry = self.ps()
                self.mmacc(pry, py[:, :TWc], [(w[:, kc, cc * 128:(cc + 1) * 128], OG[:, kc, :TWc])
                                              for kc in range(KC)], [rw, rOG])
                pa, pra = self.proj_fm(wga, rga, cc, TWc)
                op(act, lambda k=k, pa=pa: nc.scalar.activation(out=SGT[k][:, :TWc], in_=pa[:, :TWc],
                                                                func=AF.Tanh, scale=0.5), [pra], [rSGT[k]])
                op(dve, lambda k=k, c=c, py=py: nc.vector.scalar_tensor_tensor(
                    out=self.M1[:, c, :TWc], in0=SGT[k][:, :TWc], scalar=1.0, in1=py[:, :TWc],
                    op0=ALU.add, op1=ALU.mult), [rSGT[k], pry], [self.rM1])

    def lru(self, b, d, TWc, mode, t0, RL):
        nc = self.nc
        pe, act, dve, pool = self.pe, self.act, self.dve, self.pool
        op = self.op
        with ExitStack() as st:
            sb = lambda n, sh, dt_: self.sb("l_" + n, sh, dt_, st)
            XL = [sb("xl%d" % i, [128, 512], F32) for i in range(4)]
            XC = [sb("xc%d" % i, [128, 512], F32) for i in range(2)]
            XCB = [sb("xcb%d" % i, [128, 512], BF16) for i in range(2)]
            TR = [sb("tr%d" % i, [128, 512], F32) for i in range(2)]
            TI = [sb("ti%d" % i, [128, 512], F32) for i in range(2)]
            AA = [sb("a%d" % i, [128, 512], F32) for i in range(2)]
            E2 = [sb("e2%d" % i, [128, 512], F32) for i in range(2)]
            UU = [sb("u%d" % i, [128, 512], F32) for i in range(2)]
            GL = [sb("gl%d" % i, [128, 512], F32) for i in range(2)] if mode == "p2" else None
            rl = lambda n: [Res() for _ in range(n)]
            rXL, rXC, rXCB, rTR, rTI, rAA, rE2, rUU, rGL = rl(4), rl(2), rl(2), rl(2), rl(2), rl(2), rl(2), rl(2), rl(2)
            CAR, rCAR = self.CARRY[d], self.rCARRY[d]
            for xb in range(2):
                wx, rwx = self.win_block("xl", xb)
                for cc in range(4):
                    px, prx = self.proj_fm(wx, rwx, cc, TWc)
                    op(act, lambda: nc.scalar.copy(out=XL[cc][:, :TWc], in_=px[:, :TWc]), [prx], [rXL[cc]])
                yield
                wy = rwy = None

                def a1(cc, k):
                    n = xb * 4 + cc
                    op(dve, lambda: nc.vector.tensor_scalar(
                        out=XC[k][:, :TWc], in0=XL[cc][:, :TWc], scalar1=self.vcol("cw", 2 * 8 + n),
                        scalar2=self.vcol("cb", n), op0=ALU.mult, op1=ALU.add), [rXL[cc], self.rVECS], [rXC[k]])
                    xl3 = XL[cc][:, :TWc].rearrange("p (r t) -> p r t", t=RL)
                    xc3 = XC[k][:, :TWc].rearrange("p (r t) -> p r t", t=RL)
                    for j, o in ((0, -2), (1, -1), (3, 1)):
                        dsl = slice(max(0, -o), RL - max(0, o))
                        ssl = slice(max(0, o), RL - max(0, -o))
                        op(dve, lambda: nc.vector.scalar_tensor_tensor(
                            out=xc3[:, :, dsl], in0=xl3[:, :, ssl], scalar=self.vcol("cw", j * 8 + n),
                            in1=xc3[:, :, dsl], op0=ALU.mult, op1=ALU.add), [rXL[cc], rXC[k], self.rVECS], [rXC[k]])
                    op(act, lambda: nc.scalar.copy(out=XCB[k][:, :TWc], in_=XC[k][:, :TWc]), [rXC[k]], [rXCB[k]])

                def a2(cc, k):
                    n = xb * 4 + cc
                    q = k
                    pr_, rr_ = self.ps()
                    self.mmacc(rr_, pr_[:, :TWc], [(self.LRUW[:, (0 * 2 + d) * 8 + n, :], XCB[k][:, :TWc])],
                               [self.rLRUW, rXCB[k]])
                    pi_, ri_ = self.ps()
                    self.mmacc(ri_, pi_[:, :TWc], [(self.LRUW[:, (1 * 2 + d) * 8 + n, :], XCB[k][:, :TWc])],
                               [self.rLRUW, rXCB[k]])
                    dn = d * 8 + n
                    op(act, lambda: nc.scalar.activation(out=TR[k][:, :TWc], in_=pr_[:, :TWc], func=AF.Tanh,
                                                         bias=self.HBIAS[:, dn:dn + 1], scale=0.5),
                       [rr_, self.rNEGC], [rTR[k]])
                    op(act, lambda: nc.scalar.activation(out=TI[k][:, :TWc], in_=pi_[:, :TWc], func=AF.Tanh,
                                                         bias=self.HBIAS[:, 16 + dn:16 + dn + 1], scale=0.5),
                       [ri_, self.rNEGC], [rTI[k]])
                    op(act, lambda: nc.scalar.activation(out=AA[q][:, :TWc], in_=TR[k][:, :TWc], func=AF.Exp,
                                                         bias=self.NEGC2[:, dn:dn + 1], scale=self.NEGC2[:, dn:dn + 1]),
                       [rTR[k], self.rNEGC], [rAA[q]])
                    op(act, lambda: nc.scalar.activation(out=E2[q][:, :TWc], in_=TR[k][:, :TWc], func=AF.Exp,
                                                         bias=self.NEGC[:, dn:dn + 1], scale=self.NEGC[:, dn:dn + 1]),
                       [rTR[k], self.rNEGC], [rE2[q]])
                    op(act, lambda: nc.scalar.activation(out=E2[q][:, :TWc], in_=E2[q][:, :TWc], func=AF.Relu,
                                                         bias=self.EPSB[:, 2:3], scale=-0.25),
                       [rE2[q], self.rEPSB], [rE2[q]])
                    op(dve, lambda: nc.vector.scalar_tensor_tensor(
                        out=UU[q][:, :TWc], in0=TI[k][:, :TWc], scalar=1.0, in1=XC[k][:, :TWc],
                        op0=ALU.add, op1=ALU.mult), [rTI[k], rXC[k]], [rUU[q]])

                def b1(cc, q):
                    n = xb * 4 + cc
                    op(act, lambda: nc.scalar.activation(out=E2[q][:, :TWc], in_=E2[q][:, :TWc], func=AF.Sqrt),
                       [rE2[q]], [rE2[q]])
                    op(dve, lambda: nc.vector.tensor_tensor(out=UU[q][:, :TWc], in0=UU[q][:, :TWc],
                                                            in1=E2[q][:, :TWc], op=ALU.mult),
                       [rUU[q], rE2[q]], [rUU[q]])
                    kh = self.cnt2 % 2
                    self.cnt2 += 1
                    HS, rHS = self.HSC[kh], self.rHSC[kh]
                    if d == 0:
                        o_ap, a_ap, u_ap = HS[:, 0:TWc], AA[q][:, 0:TWc], UU[q][:, 0:TWc]
                        last = TWc - 1
                    else:
                        rv = slice(TWc - 1, None, -1)
                        o_ap, a_ap, u_ap = HS[:, rv], AA[q][:, rv], UU[q][:, rv]
                        last = 0
                    op(dve, lambda: nc.vector.tensor_tensor_scan(
                        out=o_ap, data0=a_ap, data1=u_ap, initial=CAR[:, n:n + 1], op0=ALU.mult, op1=ALU.add),
                        [rAA[q], rUU[q], rCAR], [rHS])
                    op(dve, lambda: nc.vector.tensor_copy(out=CAR[:, n:n + 1], in_=HS[:, last:last + 1]),
                       [rHS], [rCAR])
                    if mode == "p1":
                        self.dma(pool, self.chHSC[kh], self.hfs[b][:, n, t0:t0 + TWc], HS[:, :TWc],
                                 [rHS], [self.dr(("hfs", b, t0, n))])
                    elif mode == "p2":
                        self.dma(pool, self.chHFLC[kh], self.HFLC[kh][:, :TWc], self.hfs[b][:, n, t0:t0 + TWc],
                                 [self.dr(("hfs", b, t0, n))], [self.rHFLC[kh]])
                        op(dve, lambda: nc.vector.tensor_tensor(
                            out=HS[:, :TWc], in0=HS[:, :TWc], in1=self.HFLC[kh][:, :TWc], op=ALU.add),
                            [rHS, self.rHFLC[kh]], [rHS])
                    return HS, rHS

                def c1(cc, k, HS, rHS):
                    n = xb * 4 + cc
                    py, pry = self.proj_fm(wy, rwy, cc, TWc)
                    op(act, lambda: nc.scalar.activation(out=GL[k][:, :TWc], in_=py[:, :TWc],
                                                         func=AF.Gelu_apprx_tanh), [pry], [rGL[k]])
                    op(dve, lambda: nc.vector.tensor_tensor(
                        out=self.HL[:, n, :TWc], in0=HS[:, :TWc], in1=GL[k][:, :TWc], op=ALU.mult),
                        [rHS, rGL[k]], [self.rHL])

                for pair in range(2):
                    a1(pair * 2, 0)
                    yield
                    a1(pair * 2 + 1, 1)
                    yield
                    a2(pair * 2, 0)
                    yield
                    a2(pair * 2 + 1, 1)
                    yield
                    hs = []
                    for i in range(2):
                        hs.append(b1(pair * 2 + i, i))
                        yield
                    if mode == "p2":
                        if wy is None:
                            wy, rwy = self.win_block("yl", xb)
                        for i in range(2):
                            c1(pair * 2 + i, i, *hs[i])
                            yield
            self.barrier()

    def mix_out(self, b, TWc):
        nc = self.nc
        act, dve = self.act, self.dve
        op = self.op
        with ExitStack() as st:
            M2 = self.sb("mo_m2", [128, KC, 512], BF16, st)
            SGT = [self.sb("mo_sg%d" % i, [128, 512], F32, st) for i in range(2)]
            rM2, rSGT = Res(), [Res(), Res()]
            for ob in range(2):
                w, rw = self.wp_block(1, ob)
                wgb, rgb = self.win_block("gb", ob)
                for cc in range(4):
                    c = ob * 4 + cc
                    k = c % 2
                    py, pry = self.ps()
                    self.mmacc(pry, py[:, :TWc], [(w[:, kc, cc * 128:(cc + 1) * 128], self.HL[:, kc, :TWc])
                                                  for kc in range(KC)], [rw, self.rHL])
                    pb, prb = self.proj_fm(wgb, rgb, cc, TWc)
                    op(act, lambda k=k, pb=pb: nc.scalar.activation(out=SGT[k][:, :TWc], in_=pb[:, :TWc],
                                                                    func=AF.Tanh, scale=0.5), [prb], [rSGT[k]])
                    op(dve, lambda k=k, c=c, py=py: nc.vector.scalar_tensor_tensor(
                        out=M2[:, c, :TWc], in0=SGT[k][:, :TWc], scalar=1.0, in1=py[:, :TWc],
                        op0=ALU.add, op1=ALU.mult), [rSGT[k], pry], [rM2])
            for ob in range(2):
                w, rw = self.wp_block(2, ob)
                for cc in range(4):
                    c = ob * 4 + cc
                    po, pro = self.ps()
                    pairs = [(w[:, kc, cc * 128:(cc + 1) * 128], self.M1[:, kc, :TWc]) for kc in range(KC)] + \
                            [(w[:, kc, cc * 128:(cc + 1) * 128], M2[:, kc, :TWc]) for kc in range(KC)]
                    self.mmacc(pro, po[:, :TWc], pairs, [rw, self.rM1, rM2])
                    op(dve, lambda c=c, po=po: nc.vector.scalar_tensor_tensor(
                        out=self.H[:, c, :TWc], in0=po[:, :TWc], scalar=self.GS[:, 1, c, b:b + 1],
                        in1=self.H[:, c, :TWc], op0=ALU.mult, op1=ALU.add), [pro, self.rGS, self.rH], [self.rH])
            self.barrier()

    def final_norm(self, TWc):
        nc = self.nc
        with ExitStack() as st:
            RS, rRS = self.norm_stats(st, TWc, KC, self.H[:, :, :TWc], self.rH, 1.0 / D, "fn")
            for c in range(KC):
                self.op(self.dve, lambda c=c: nc.vector.scalar_tensor_tensor(
                    out=self.HOUT[:, c, :TWc], in0=self.H[:, c, :TWc], scalar=self.vcol("fnw", c), in1=RS[:, :TWc],
                    op0=ALU.mult, op1=ALU.mult), [self.rH, rRS, self.rVECS], [self.rM1, self.rHL])
            self.barrier()

    def batch(self, b):
        nc, NB, T, TC, TW = self.nc, self.NB, self.T, self.TC, self.TW
        pool, dve = self.pool, self.dve
        op, dma = self.op, self.dma
        if b == 0:
            self.EPSB = self.sb("epsb", [128, 3], F32)
            self.ONEB = self.EPSB[:, 1:2]
            self.rEPSB = Res()
            op(dve, lambda: nc.vector.memset(self.EPSB[:, 0:1], EPS), [], [self.rEPSB])
            op(dve, lambda: nc.vector.memset(self.EPSB[:, 1:2], 1.0), [], [self.rEPSB])
            op(dve, lambda: nc.vector.memset(self.EPSB[:, 2:3], 0.25), [], [self.rEPSB])
        for d in range(2):
            op(dve, lambda d=d: nc.vector.memset(self.S32[d][:, :, :], 0.0), [], [self.rS32[d]])
            op(dve, lambda d=d: nc.vector.memset(self.CARRY[d][:, :], 0.0), [], [self.rCARRY[d]])
        if True:
            dma(pool, self.chH, self.H[:, :, :TC], self.cxT[b], [], [self.rH])
            self.ffn(0, 0, NB, TC)
        else:
            dma(pool, self.chH, self.H[:, :, :TC], self.ch1s[b], [self.dr(("ch1s", b))], [self.rH])
        self.norm_mod(1, NB, TC)
        def m_ctx():
            for d in range(2):
                yield from self.gla(b, d, TC, "ctx", 0, coro=self.lru(b, d, TC, "ctx", 0, TC))
        tiles = list(range(0, T, TW))
        self.pass1(b, tiles, m_ctx(), self.deferred)
        self.deferred = []
        with ExitStack() as st2:
            MH = self.sb("mh_sb", [128, 2 * KC * 512], BF16, st2)
            self.M1 = MH[:, 0:KC * 512].rearrange("p (c t) -> p c t", c=KC)
            self.HL = MH[:, KC * 512:2 * KC * 512].rearrange("p (c t) -> p c t", c=KC)
            self.HOUT = MH[:, :].bitcast(F32).rearrange("p (c t) -> p c t", c=KC)
            for t0 in reversed(tiles):
                dma(pool, self.chH, self.H[:, :, :TW], self.h1s[b][:, :, t0:t0 + TW], [self.dr(("h1s", b, t0))],
                    [self.rH])
                self.norm_mod(1, b, TW)
                for _ in self.gla(b, 1, TW, "p2", t0, coro=self.lru(b, 1, TW, "p2", t0, 64)):
                    pass
                self.mix_out(b, TW)
                if b + 1 < NB and t0 in tiles[:2] and len(tiles) >= 4:
                    dma(pool, self.chH2, self.h2s[b][:, :, t0:t0 + TW], self.H[:, :, :TW], [self.rH],
                        [self.dr(("h2s", b, t0))])
                    self.deferred.append((b, t0))
                    continue
                self.ffn(1, 2, b, TW)
                self.final_norm(TW)
                dma(pool, self.chHst, self.outT[b][:, :, t0:t0 + TW], self.HOUT[:, :, :TW], [self.rM1, self.rHL],
                    [self.dr(("out", b, t0))])
            for e in (self.pe, self.act, self.dve, self.pool):
                e.b.wait_ge(self.chHst.sem, self.chHst.cnt)
                e.seen[self.chHst] = self.chHst.cnt
            self.barrier()


def _fm(v):
    v = np.asarray(v, np.float32)
    return np.ascontiguousarray(v.reshape(-1, 128).T)


def _consts():
    c = np.zeros((128, NCONST), np.float32)
    j = np.arange(128)[:, None]
    i = np.arange(128)[None, :]
    same = (j // 64) == (i // 64)
    c[:, 0:128] = np.where(same & (j <= i), -1.0 / 16, 0.0)
    c[:, 128:256] = np.where(same & (j >= i), -1.0 / 16, 0.0)
    c[:, 256:384] = np.where(same & (j > i), -1.0 / 16, 0.0)
    c[:, 384:512] = np.where(same & (j < i), -1.0 / 16, 0.0)
    m0 = np.where(same & (j <= i), 1.0, 0.0)
    m1 = np.where(same & (j > i), 1.0, 0.0)
    c[:, 512:640] = m0
    c[:, 640:768] = m1
    return c


def _to_fm_tokens(a):
    nb, t, _ = a.shape
    return np.ascontiguousarray(a.reshape(nb, t, KC, 128).transpose(0, 3, 2, 1))


def _from_fm_tokens(a):
    nb, _, _, t = a.shape
    return np.ascontiguousarray(a.transpose(0, 3, 2, 1).reshape(nb, t, D))


_NC_CACHE = {}


def run(inputs, n_cores, trace=False):
    x = np.asarray(inputs["x"], np.float32)
    ctx = np.asarray(inputs["ctx"], np.float32)
    c = np.asarray(inputs["c"], np.float32)
    B, T, _ = x.shape
    TC = ctx.shape[1]
    NB = B // n_cores
    key = (NB, T, TC)
    prog = Prog(NB, T, TC)
    nc = prog.build()
    g = lambda k: np.asarray(inputs[k], np.float32)
    vecs = np.zeros((128, NVEC), np.float32)

    def put(name, arr):
        a = _fm(arr)
        vecs[:, VC[name]:VC[name] + a.shape[1]] = a
    put("nw", g("norm_w")[0])
    put("bada", g("b_ada")[0])
    put("cw", g("conv_w")[0])
    put("cb", g("conv_b")[0])
    put("lbr", g("lru_br")[0])
    put("lbi", g("lru_bi")[0])
    put("lam", g("lru_lam")[0])
    put("gnw", g("gla_norm_w")[0])
    put("fnw", g("final_norm_w"))
    fupa = np.zeros((32, 2, 512), np.float32)
    fupa[0:16] = g("gla_fup")[0].transpose(1, 0, 2)
    fupa[16] = g("gla_fb")[0]
    shared = {
        "vecs": vecs, "consts": _consts(), "fupa": fupa,
        "w_ada": np.ascontiguousarray(g("w_ada")[0]),
        "ffn1_wi": np.ascontiguousarray(g("ffn1_wi")[0]), "ffn1_wo": np.ascontiguousarray(g("ffn1_wo")[0]),
        "ffn2_wi": np.ascontiguousarray(g("ffn2_wi")[0]), "ffn2_wo": np.ascontiguousarray(g("ffn2_wo")[0]),
        "w_in": np.ascontiguousarray(g("w_in")[0]),
        "lru_wr": np.ascontiguousarray(g("lru_wr")[0]), "lru_wi": np.ascontiguousarray(g("lru_wi")[0]),
        "w_out_gla": np.ascontiguousarray(g("w_out_gla")[0]), "w_out_lru": np.ascontiguousarray(g("w_out_lru")[0]),
        "w_o": np.ascontiguousarray(g("w_o")[0]),
    }
    c_ctx = g("c_ctx")
    in_maps = []
    for i in range(n_cores):
        sl = slice(i * NB, (i + 1) * NB)
        cc = np.concatenate([c[sl], c_ctx[None, :]], axis=0)
        cT = np.ascontiguousarray(cc.reshape(NB + 1, KC, 128).transpose(2, 1, 0))
        m = dict(shared)
        m["xT"] = _to_fm_tokens(x[sl])
        m["cxT"] = _to_fm_tokens(ctx[sl])
        m["cT"] = cT
        in_maps.append(m)
    res = run_bass_kernel_spmd(nc, in_maps, core_ids=list(range(n_cores)), trace=trace)
    out = np.concatenate([_from_fm_tokens(np.asarray(r["outT"])) for r in res.results], axis=0)
    return out.astype(np.float32), res


def kernel(**inputs):
    out, _ = run(inputs, 8)
    return out
```

```python
import numpy as np
from contextlib import ExitStack
import concourse.bass as bass
import concourse.mybir as mybir
from concourse.bass_utils import run_bass_kernel_spmd

F32 = mybir.dt.float32
BF16 = mybir.dt.bfloat16
AF = mybir.ActivationFunctionType
ALU = mybir.AluOpType

D = 1024
KC = 8
FH = 2816
HC = 22
EPS = 1e-6
NSLOT = 4
WIN_BLK = {"q": [0], "k": [512], "v": [1024, 1536], "g": [2048, 2560], "xl": [3104, 3616],
           "yl": [4128, 4640], "ga": [5152, 5664], "gb": [6176, 6688]}
WIN_ORDER = ["q", "k", "v", "g", "xl", "yl", "ga", "gb"]
WIN_IDX = {}
_i = 0
for _n in WIN_ORDER:
    for _j in range(len(WIN_BLK[_n])):
        WIN_IDX[(_n, _j)] = _i
        _i += 1
N_WIN_BLK = _i

VC = {}
_o = 0
for _n, _w in [("nw", 24), ("bada", 72), ("cw", 32), ("cb", 8), ("lbr", 16), ("lbi", 16), ("lam", 16),
               ("gnw", 2), ("fnw", 8)]:
    VC[_n] = _o
    _o += _w
NVEC = _o
CC = {"tri0": 0, "tri1": 128, "utri0": 256, "utri1": 384, "mask0": 512, "mask1": 640}
NCONST = 768


class Eng:
    def __init__(self, name, b, sem):
        self.name, self.b, self.sem, self.cnt, self.seen = name, b, sem, 0, {}


class Chan:
    def __init__(self, sem):
        self.sem, self.cnt = sem, 0


class Res:
    __slots__ = ("w", "r")

    def __init__(self):
        self.w = None
        self.r = {}


class ResGroup:
    def __init__(self, n):
        self.parts = [Res() for _ in range(n)]


def _flat(rs):
    out = []
    for r in rs:
        if isinstance(r, ResGroup):
            out.extend(r.parts)
        else:
            out.append(r)
    return out


class Prog:
    def __init__(self, NB, T, TC, TW=512):
        self.NB, self.T, self.TC, self.TW = NB, T, TC, TW
        self.NBC = NB + 1
        self.nc = bass.Bass("TRN2", target_bir_lowering=False)
        self.es = ExitStack()
        self.nsem = 0
        self.dres = {}
        self.sp_jobs = []

    def sem(self):
        self.nsem += 1
        return self.es.enter_context(self.nc.semaphore("s%d" % self.nsem))

    def chan(self):
        return Chan(self.sem())

    def sb(self, name, shape, dt, st=None):
        self.nsb = getattr(self, "nsb", 0) + 1
        stack = st if st is not None else self.es
        return stack.enter_context(self.nc.sbuf_tensor("%s_%d" % (name, self.nsb), shape, dt))

    def dr(self, key):
        r = self.dres.get(key)
        if r is None:
            r = self.dres[key] = Res()
        return r

    def _need(self, eng, reads, writes):
        reads, writes = _flat(reads), _flat(writes)
        need = {}

        def add(sc):
            if sc is None:
                return
            s, c = sc
            if need.get(s, 0) < c:
                need[s] = c
        for r in reads:
            add(r.w)
        for w in writes:
            add(w.w)
            for s, c in w.r.items():
                add((s, c))
        out = []
        for s, c in need.items():
            if s is eng and eng.name == "pe":
                continue
            if eng.seen.get(s, 0) >= c:
                continue
            eng.seen[s] = c
            out.append((s, c))
        return out

    def _waits(self, eng, reads, writes):
        for s, c in self._need(eng, reads, writes):
            eng.b.wait_ge(s.sem, c)

    @staticmethod
    def _commit(src, cnt, reads, writes):
        reads, writes = _flat(reads), _flat(writes)
        for r in reads:
            r.r[src] = cnt
        for w in writes:
            w.w = (src, cnt)
            w.r = {}

    def op(self, eng, fn, reads, writes):
        self._waits(eng, reads, writes)
        ins = fn()
        eng.cnt += 1
        ins.then_inc(eng.sem, 1)
        self._commit(eng, eng.cnt, reads, writes)

    def mm(self, ps_res, items, reads):
        pe = self.pe
        self._waits(pe, reads, [ps_res])
        n = len(items)
        ins = None
        for i, (o, l, r, st) in enumerate(items):
            stop = (i == n - 1) or items[i + 1][3]
            ins = self.nc.tensor.matmul(o, lhsT=l, rhs=r, start=st, stop=stop)
        pe.cnt += 1
        ins.then_inc(pe.sem, 1)
        self._commit(pe, pe.cnt, reads, [ps_res])

    def mmacc(self, ps_res, out, pairs, reads):
        self.mm(ps_res, [(out, l, r, i == 0) for i, (l, r) in enumerate(pairs)], reads)

    def dma(self, q, chan, out, in_, reads, writes):
        self._waits(q, reads, writes)
        q.b.dma_start(out=out, in_=in_).then_inc(chan.sem, 16)
        chan.cnt += 16
        self._commit(chan, chan.cnt, reads, writes)

    def barrier(self):
        es = [self.pe, self.act, self.dve, self.pool]
        for e in es:
            for f in es:
                if f is e or f.cnt == 0:
                    continue
                if e.seen.get(f, 0) >= f.cnt:
                    continue
                e.seen[f] = f.cnt
                e.b.wait_ge(f.sem, f.cnt)

    def ps(self):
        i = self.ps_i
        self.ps_i = (i + 1) % 8
        return self.pst[i], self.psr[i]

    def wnext(self, src_ap, L, grp_res, ndma_parts=None):
        i = self.w_i
        self.w_i += 1
        k = i % NSLOT
        slot, res, ch = self.wslot[k], self.wres[k], self.wchan[k]
        need = self._need(self.sp, [grp_res], [res])
        self.sp_jobs.append((need, slot, src_ap, L, ch))
        ch.cnt += 16
        self._commit(ch, ch.cnt, [grp_res], [res])
        return slot, res

    def flush_sp(self):
        sp = self.sp
        for need, slot, src, L, ch in self.sp_jobs:
            for s, c in need:
                sp.b.wait_ge(s.sem, c)
            sp.b.dma_start(out=slot[:, 0:L], in_=src).then_inc(ch.sem, 16)
        self.sp_jobs = []

    def build(self):
        nc, NB, T, TC, NBC = self.nc, self.NB, self.T, self.TC, self.NBC
        es = self.es
        dt = nc.dram_tensor
        self.xT = dt("xT", [NB, 128, KC, T], F32, kind="ExternalInput").ap()
        self.cxT = dt("cxT", [NB, 128, KC, TC], F32, kind="ExternalInput").ap()
        self.cT = dt("cT", [128, KC, NBC], F32, kind="ExternalInput").ap()
        self.vecs_d = dt("vecs", [128, NVEC], F32, kind="ExternalInput").ap()
        self.consts_d = dt("consts", [128, NCONST], F32, kind="ExternalInput").ap()
        self.fupa_d = dt("fupa", [32, 2, 512], F32, kind="ExternalInput").ap()
        self.w_ada = dt("w_ada", [D, 9 * D], F32, kind="ExternalInput").ap()
        self.wi_d = [dt("ffn%d_wi" % (f + 1), [D, 2 * FH], F32, kind="ExternalInput").ap() for f in range(2)]
        self.wo_d = [dt("ffn%d_wo" % (f + 1), [FH, D], F32, kind="ExternalInput").ap() for f in range(2)]
        self.w_in = dt("w_in", [D, 7200], F32, kind="ExternalInput").ap()
        self.lru_w_d = [dt(n, [2, 8, 128, 128], F32, kind="ExternalInput").ap() for n in ("lru_wr", "lru_wi")]
        self.wp_d = [dt(n, [D, D], F32, kind="ExternalInput").ap() for n in ("w_out_gla", "w_out_lru", "w_o")]
        self.outT = dt("outT", [NB, 128, KC, T], F32, kind="ExternalOutput").ap()
        self.s_wi = [dt("s_wi%d" % f, [11, 128, 4096], BF16, kind="Internal").ap() for f in range(2)]
        self.s_wo = [dt("s_wo%d" % f, [8, 128, HC * 128], BF16, kind="Internal").ap() for f in range(2)]
        self.s_win = dt("s_win", [N_WIN_BLK, 128, 4096], BF16, kind="Internal").ap()
        self.s_wp = dt("s_wp", [6, 128, 4096], BF16, kind="Internal").ap()
        self.h1s = dt("h1s", [NB, 128, KC, T], F32, kind="Internal").ap()
        self.ofs = dt("ofs", [NB, 128, KC, T], F32, kind="Internal").ap()
        self.hfs = dt("hfs", [NB, 128, KC, T], F32, kind="Internal").ap()
        self.ch1s = dt("ch1s", [NB, 128, KC, TC], F32, kind="Internal").ap()
        self.h2s = dt("h2s", [NB, 128, KC, T], F32, kind="Internal").ap()
        self.deferred = []

        self.pe = Eng("pe", nc.tensor, self.sem())
        self.act = Eng("act", nc.scalar, self.sem())
        self.dve = Eng("dve", nc.vector, self.sem())
        self.pool = Eng("pool", nc.gpsimd, self.sem())
        self.sp = Eng("sp", nc.sync, None)
        pe, act, dve, pool = self.pe, self.act, self.dve, self.pool

        self.pst = [es.enter_context(nc.psum_tensor("ps%d" % i, [128, 512], F32)) for i in range(8)]
        self.psr = [Res() for _ in range(8)]
        self.ps_i = 0
        self.wslot = [self.sb("wslot%d" % i, [128, 4096], BF16) for i in range(NSLOT)]
        self.wres = [Res() for _ in range(NSLOT)]
        self.wchan = [self.chan() for _ in range(NSLOT)]
        self.w_i = 0

        sb = self.sb
        self.VECS = sb("vecs_sb", [128, NVEC], F32)
        self.CONSTS = sb("consts_sb", [128, NCONST], F32)
        self.FUPA = sb("fupa_sb", [32, 2, 512], BF16)
        self.ONES = sb("ones_sb", [128, 128], BF16)
        self.LRUW = sb("lruw_sb", [128, 32, 128], BF16)
        self.FDW = sb("fdw_sb", [128, KC, 32], BF16)
        self.MODT = sb("modt_sb", [128, 72, NBC], F32)
        self.AS = sb("as_sb", [128, 3, KC, NBC], F32)
        self.GS = sb("gs_sb", [128, 3, KC, NBC], F32)
        self.NEGC = sb("negc_sb", [128, 16], F32)
        self.NEGC2 = sb("negc2_sb", [128, 16], F32)
        self.HBIAS = sb("hbias_sb", [128, 32], F32)
        self.H = sb("h_sb", [128, KC, 512], F32)
        self.U = sb("u_sb", [128, KC, 512], BF16)
        self.FD = sb("fd_sb", [32, 512], BF16)
        self.S32 = [sb("s32_%d" % d, [128, 4, 256], F32) for d in range(2)]
        self.SBF = [sb("sbf_%d" % i, [128, 4, 256], BF16) for i in range(3)]
        self.CARRY = [sb("carry_%d" % d, [128, KC], F32) for d in range(2)]
        self.OB = [sb("ob_%d" % i, [128, KC, 128], F32) for i in range(1)]
        self.OFL = self.OB
        self.HSC = [sb("hsc_%d" % i, [128, 512], F32) for i in range(2)]
        self.HFLC = [sb("hflc_%d" % i, [128, 512], F32) for i in range(2)]
        R = Res
        self.rVECS, self.rCONSTS, self.rFUPA, self.rONES, self.rLRUW, self.rFDW = R(), R(), R(), R(), R(), R()
        self.rMODT, self.rAS, self.rGS, self.rNEGC = R(), R(), R(), R()
        self.rH, self.rU, self.rFD = R(), ResGroup(KC), R()
        self.rS32 = [ResGroup(4), ResGroup(4)]
        self.rSBF = [R(), R(), R()]
        self.rCARRY = [R(), R()]
        self.rM1, self.rHL = R(), R()
        self.rOB, self.rHSC, self.rHFLC = [R(), R()], [R(), R()], [R(), R()]
        self.rOFL = self.rOB
        self.chH, self.chHst, self.chH2 = self.chan(), self.chan(), self.chan()
        self.chOB = [self.chan(), self.chan()]
        self.chOFL = self.chOB
        self.chHSC = [self.chan(), self.chan()]
        self.chHFLC = [self.chan(), self.chan()]
        self.sbf_i = 0
        self.cnt2 = 0

        self.prologue()
        for b in range(NB):
            self.batch(b)
        for ch in [self.chH, self.chHst, self.chH2] + self.chOB + self.chHSC + self.chHFLC:
            if ch.cnt:
                pool.b.wait_ge(ch.sem, ch.cnt)
        self.flush_sp()
        self.es.close()
        return nc

    def vcol(self, name, i):
        o = VC[name] + i
        return self.VECS[:, o:o + 1]

    def prologue(self):
        nc, NBC = self.nc, self.NBC
        pe, act, dve, pool = self.pe, self.act, self.dve, self.pool
        op, dma = self.op, self.dma
        dma(pool, self.chan(), self.VECS[:, :], self.vecs_d[:, :], [], [self.rVECS])
        dma(pool, self.chan(), self.CONSTS[:, :], self.consts_d[:, :], [], [self.rCONSTS])
        dma(pool, self.chan(), self.FUPA[:, :, :], self.fupa_d[:, :, :], [], [self.rFUPA])
        chl = self.chan()
        for g in range(2):
            dma(pool, chl, self.LRUW[:, g * 16:(g + 1) * 16, :],
                self.lru_w_d[g].rearrange("d n c m -> c (d n) m"), [], [self.rLRUW])
        dma(pool, self.chan(), self.FDW[:, :, :],
            self.w_in.rearrange("(kc p) n -> p kc n", p=128)[:, :, 3072:3104], [], [self.rFDW])
        op(dve, lambda: nc.vector.memset(self.ONES[:, :], 1.0), [], [self.rONES])
        op(dve, lambda: nc.vector.memset(self.FD[:, :], 1.0), [], [self.rFD])
        self.g_wi = [Res(), Res()]
        self.g_wo = [Res(), Res()]
        self.g_win, self.g_wp = Res(), Res()

        def cast_wi(f):
            ch = self.chan()
            src = self.wi_d[f].rearrange("(kc p) n -> p kc n", p=128)
            for jb in range(11):
                dst = self.s_wi[f][jb].rearrange("p (g k n) -> p g k n", g=2, k=KC)
                for gu in range(2):
                    c0 = gu * FH + jb * 256
                    dma(pool, ch, dst[:, gu], src[:, :, c0:c0 + 256], [], [self.g_wi[f]])

        def cast_wo(f):
            ch = self.chan()
            src = self.wo_d[f].rearrange("(j p) n -> p j n", p=128)
            for c in range(8):
                dst = self.s_wo[f][c].rearrange("p (j m) -> p j m", j=HC)
                dma(pool, ch, dst, src[:, :, c * 128:(c + 1) * 128], [], [self.g_wo[f]])

        def cast_win():
            ch = self.chan()
            src = self.w_in.rearrange("(kc p) n -> p kc n", p=128)
            for n in WIN_ORDER:
                for j, c0 in enumerate(WIN_BLK[n]):
                    dst = self.s_win[WIN_IDX[(n, j)]].rearrange("p (k n) -> p k n", k=KC)
                    dma(pool, ch, dst, src[:, :, c0:c0 + 512], [], [self.g_win])

        def cast_wp():
            ch = self.chan()
            for w in range(3):
                src = self.wp_d[w].rearrange("(kc p) n -> p kc n", p=128)
                for ob in range(2):
                    dst = self.s_wp[w * 2 + ob].rearrange("p (k n) -> p k n", k=KC)
                    dma(pool, ch, dst, src[:, :, ob * 512:(ob + 1) * 512], [], [self.g_wp])

        cast_wi(0)
        cast_wo(0)
        with ExitStack() as st:
            CTs = self.sb("ct_sb", [128, KC, NBC], F32, st)
            SCs = self.sb("sc_sb", [128, KC, NBC], BF16, st)
            WA = [self.sb("wa_%d" % i, [128, KC, 512], BF16, st) for i in range(2)]
            rCT, rSC, rWA = Res(), Res(), [Res(), Res()]
            chw = [self.chan(), self.chan()]
            dma(pool, self.chan(), CTs[:, :, :], self.cT[:, :, :], [], [rCT])
            op(act, lambda: nc.scalar.activation(out=SCs[:, :, :], in_=CTs[:, :, :], func=AF.Silu), [rCT], [rSC])
            wsrc = self.w_ada.rearrange("(kc p) n -> p kc n", p=128)
            for blk in range(18):
                k = blk % 2
                dma(pool, chw[k], WA[k][:, :, :], wsrc[:, :, blk * 512:(blk + 1) * 512], [], [rWA[k]])
                if blk == 8:
                    cast_win()
                pt, pr = self.ps()
                for m4 in range(4):
                    self.mmacc(pr, pt[:, m4 * 8:m4 * 8 + NBC],
                               [(WA[k][:, kc, m4 * 128:(m4 + 1) * 128], SCs[:, kc, :]) for kc in range(KC)],
                               [rWA[k], rSC])
                for m4 in range(4):
                    m = blk * 4 + m4
                    op(dve, lambda m=m, m4=m4: nc.vector.tensor_scalar(
                        out=self.MODT[:, m, :], in0=pt[:, m4 * 8:m4 * 8 + NBC], scalar1=self.vcol("bada", m),
                        scalar2=None, op0=ALU.add), [pr, self.rVECS], [self.rMODT])
            M4 = self.MODT[:, :, :].rearrange("p (j c) b -> p j c b", j=9)
            for s in range(3):
                for b in range(NBC):
                    op(dve, lambda s=s, b=b: nc.vector.scalar_tensor_tensor(
                        out=self.AS[:, s, :, b], in0=M4[:, 3 * s + 1, :, b], scalar=1.0,
                        in1=self.VECS[:, VC["nw"] + s * 8:VC["nw"] + s * 8 + 8], op0=ALU.add, op1=ALU.mult),
                       [self.rMODT, self.rVECS], [self.rAS])
                gf = 0.5
                op(dve, lambda s=s, gf=gf: nc.vector.tensor_scalar(
                    out=self.GS[:, s, :, :], in0=M4[:, 3 * s + 2, :, :], scalar1=gf, scalar2=None, op0=ALU.mult),
                   [self.rMODT], [self.rGS])
            lam = self.VECS[:, VC["lam"]:VC["lam"] + 16]
            op(act, lambda: nc.scalar.activation(out=self.NEGC[:, :], in_=lam, func=AF.Exp, scale=-1.0),
               [self.rVECS], [self.rNEGC])
            op(act, lambda: nc.scalar.activation(out=self.NEGC[:, :], in_=self.NEGC[:, :], func=AF.Ln, bias=1.0),
               [self.rNEGC], [self.rNEGC])
            op(dve, lambda: nc.vector.tensor_scalar(out=self.NEGC2[:, :], in0=self.NEGC[:, :], scalar1=-4.0,
                                                    scalar2=None, op0=ALU.mult), [self.rNEGC], [self.rNEGC])
            op(dve, lambda: nc.vector.tensor_scalar(out=self.NEGC[:, :], in0=self.NEGC[:, :], scalar1=-8.0,
                                                    scalar2=None, op0=ALU.mult), [self.rNEGC], [self.rNEGC])
            op(dve, lambda: nc.vector.tensor_scalar(out=self.HBIAS[:, :], in0=self.VECS[:, VC["lbr"]:VC["lbr"] + 32],
                                                    scalar1=0.5, scalar2=None, op0=ALU.mult),
               [self.rVECS], [self.rNEGC])
            self.barrier()
        cast_wp()
        cast_wi(1)
        cast_wo(1)

    def norm_stats(self, st, TWc, nchunk, src, rsrc, inv_n, tag):
        nc = self.nc
        SQ = self.sb("sq_" + tag, [128, KC, 512], BF16, st)
        RS = self.sb("rs_" + tag, [128, 512], F32, st)
        rSQa, rSQb, rRS = Res(), Res(), Res()
        H = self.H
        self.op(self.act, lambda: nc.scalar.activation(out=SQ[:, 0:4, :TWc], in_=H[:, 0:4, :TWc], func=AF.Square),
                [self.rH], [rSQa])
        self.op(self.dve, lambda: nc.vector.tensor_tensor(out=SQ[:, 4:8, :TWc], in0=H[:, 4:8, :TWc],
                                                          in1=H[:, 4:8, :TWc], op=ALU.mult), [self.rH], [rSQb])
        pt, pr = self.ps()
        self.mmacc(pr, pt[:, :TWc], [(self.ONES[:, :], SQ[:, c, :TWc]) for c in range(KC)],
                   [self.rONES, rSQa, rSQb])
        self.op(self.act, lambda: nc.scalar.activation(out=RS[:, :TWc], in_=pt[:, :TWc], func=AF.Ln,
                                                       bias=self.EPSB[:, 0:1], scale=inv_n), [pr, self.rEPSB], [rRS])
        self.op(self.act, lambda: nc.scalar.activation(out=RS[:, :TWc], in_=RS[:, :TWc], func=AF.Exp, scale=-0.5),
                [rRS], [rRS])
        return RS, rRS

    def norm_mod(self, s, b, TWc):
        nc = self.nc
        with ExitStack() as st:
            RS, rRS = self.norm_stats(st, TWc, KC, None, None, 1.0 / D, "nm")
            T = self.sb("nm_t", [128, KC, 512], F32, st)
            rT = [Res(), Res()]
            RSb = RS[:, :TWc].unsqueeze(1).broadcast_to([128, 4, TWc])
            for hf in range(2):
                self.op(self.dve, lambda hf=hf: nc.vector.tensor_tensor(
                    out=T[:, hf * 4:hf * 4 + 4, :TWc], in0=self.H[:, hf * 4:hf * 4 + 4, :TWc], in1=RSb, op=ALU.mult),
                    [self.rH, rRS], [rT[hf]])
            for c in range(4):
                self.op(self.act, lambda c=c: nc.scalar.activation(
                    out=self.U[:, c, :TWc], in_=T[:, c, :TWc], func=AF.Identity,
                    bias=self.MODT[:, (3 * s) * 8 + c, b:b + 1], scale=self.AS[:, s, c, b:b + 1]),
                    [rT[0], self.rMODT, self.rAS], [self.rU.parts[c]])
            for c in range(4, 8):
                self.op(self.dve, lambda c=c: nc.vector.tensor_scalar(
                    out=self.U[:, c, :TWc], in0=T[:, c, :TWc], scalar1=self.AS[:, s, c, b:b + 1],
                    scalar2=self.MODT[:, (3 * s) * 8 + c, b:b + 1], op0=ALU.mult, op1=ALU.add),
                    [rT[1], self.rMODT, self.rAS], [self.rU.parts[c]])
            self.barrier()

    def ffn(self, f, s, b, TWc):
        nc = self.nc
        pe, act, dve = self.pe, self.act, self.dve
        self.norm_mod(s, b, TWc)
        with ExitStack() as st:
            ACTH = self.sb("acth", [128, HC, 512], BF16, st)
            SG = [self.sb("sg%d" % i, [128, 512], F32, st) for i in range(2)]
            rACTH, rSG = Res(), [Res(), Res()]
            for jb in range(11):
                slot, wr = self.wnext(self.s_wi[f][jb], 4096, self.g_wi[f])
                wv = slot[:, :].rearrange("p (g k n) -> p g k n", g=2, k=KC)
                for jj in range(2):
                    j = jb * 2 + jj
                    pg, rg = self.ps()
                    self.mmacc(rg, pg[:, :TWc], [(wv[:, 0, kc, jj * 128:(jj + 1) * 128], self.U[:, kc, :TWc])
                                                 for kc in range(KC)], [wr, self.rU])
                    pu, ru = self.ps()
                    self.mmacc(ru, pu[:, :TWc], [(wv[:, 1, kc, jj * 128:(jj + 1) * 128], self.U[:, kc, :TWc])
                                                 for kc in range(KC)], [wr, self.rU])
                    k = j % 2
                    self.op(act, lambda k=k, pg=pg: nc.scalar.activation(out=SG[k][:, :TWc], in_=pg[:, :TWc],
                                                                         func=AF.Silu), [rg], [rSG[k]])
                    self.op(dve, lambda k=k, pu=pu, j=j: nc.vector.tensor_tensor(
                        out=ACTH[:, j, :TWc], in0=SG[k][:, :TWc], in1=pu[:, :TWc], op=ALU.mult),
                        [rSG[k], ru], [rACTH])
            for c in range(KC):
                slot, wr = self.wnext(self.s_wo[f][c], HC * 128, self.g_wo[f])
                wv = slot[:, 0:HC * 128].rearrange("p (j m) -> p j m", j=HC)
                po, ro = self.ps()
                self.mmacc(ro, po[:, :TWc], [(wv[:, j, :], ACTH[:, j, :TWc]) for j in range(HC)], [wr, rACTH])
                self.op(dve, lambda c=c, po=po: nc.vector.scalar_tensor_tensor(
                    out=self.H[:, c, :TWc], in0=po[:, :TWc], scalar=self.GS[:, s, c, b:b + 1],
                    in1=self.H[:, c, :TWc], op0=ALU.mult, op1=ALU.add), [ro, self.rGS, self.rH], [self.rH])
            self.barrier()

    def norm_mod_p(self, s, b, TWc, P, Uout, rUout):
        nc = self.nc
        H, SQ, RS, TMP = self.H, P["ACTH"], P["RS"], P["TMP"]
        rSQ, rRS, rTMP = P["rACTH"], P["rRS"], P["rTMP"]
        self.op(self.act, lambda: nc.scalar.activation(out=SQ[:, 0:4, :TWc], in_=H[:, 0:4, :TWc], func=AF.Square),
                [self.rH], [rSQ])
        self.op(self.dve, lambda: nc.vector.tensor_tensor(out=SQ[:, 4:8, :TWc], in0=H[:, 4:8, :TWc],
                                                          in1=H[:, 4:8, :TWc], op=ALU.mult), [self.rH], [rSQ])
        pt, pr = self.ps()
        self.mmacc(pr, pt[:, :TWc], [(self.ONES[:, :], SQ[:, c, :TWc]) for c in range(KC)], [self.rONES, rSQ])
        self.op(self.act, lambda: nc.scalar.activation(out=RS[:, :TWc], in_=pt[:, :TWc], func=AF.Ln,
                                                       bias=self.EPSB[:, 0:1], scale=1.0 / D), [pr, self.rEPSB], [rRS])
        self.op(self.act, lambda: nc.scalar.activation(out=RS[:, :TWc], in_=RS[:, :TWc], func=AF.Exp, scale=-0.5),
                [rRS], [rRS])
        for c in range(KC):
            k = c % 2
            self.op(self.dve, lambda: nc.vector.tensor_tensor(
                out=TMP[k][:, :TWc], in0=H[:, c, :TWc], in1=RS[:, :TWc], op=ALU.mult), [self.rH, rRS], [rTMP[k]])
            if c % 2 == 0:
                self.op(self.act, lambda: nc.scalar.activation(
                    out=Uout[:, c, :TWc], in_=TMP[k][:, :TWc], func=AF.Identity,
                    bias=self.MODT[:, (3 * s) * 8 + c, b:b + 1], scale=self.AS[:, s, c, b:b + 1]),
                    [rTMP[k], self.rMODT, self.rAS], [rUout.parts[c]])
            else:
                self.op(self.dve, lambda: nc.vector.tensor_scalar(
                    out=Uout[:, c, :TWc], in0=TMP[k][:, :TWc], scalar1=self.AS[:, s, c, b:b + 1],
                    scalar2=self.MODT[:, (3 * s) * 8 + c, b:b + 1], op0=ALU.mult, op1=ALU.add),
                    [rTMP[k], self.rMODT, self.rAS], [rUout.parts[c]])

    def ffn_gen(self, f, s, b, TWc, P):
        nc = self.nc
        act, dve = self.act, self.dve
        UF, rUF, ACTH, rACTH, SG, rSG = P["UF"], P["rUF"], P["ACTH"], P["rACTH"], P["SG"], P["rSG"]
        self.norm_mod_p(s, b, TWc, P, UF, rUF)
        yield
        for jb in range(11):
            slot, wr = self.wnext(self.s_wi[f][jb], 4096, self.g_wi[f])
            wv = slot[:, :].rearrange("p (g k n) -> p g k n", g=2, k=KC)
            for jj in range(2):
                j = jb * 2 + jj
                pg, rg = self.ps()
                self.mmacc(rg, pg[:, :TWc], [(wv[:, 0, kc, jj * 128:(jj + 1) * 128], UF[:, kc, :TWc])
                                             for kc in range(KC)], [wr, rUF])
                pu, ru = self.ps()
                self.mmacc(ru, pu[:, :TWc], [(wv[:, 1, kc, jj * 128:(jj + 1) * 128], UF[:, kc, :TWc])
                                             for kc in range(KC)], [wr, rUF])
                k = j % 2
                self.op(act, lambda: nc.scalar.activation(out=SG[k][:, :TWc], in_=pg[:, :TWc], func=AF.Silu),
                        [rg], [rSG[k]])
                self.op(dve, lambda: nc.vector.tensor_tensor(
                    out=ACTH[:, j, :TWc], in0=SG[k][:, :TWc], in1=pu[:, :TWc], op=ALU.mult),
                    [rSG[k], ru], [rACTH])
            yield
        for c in range(KC):
            slot, wr = self.wnext(self.s_wo[f][c], HC * 128, self.g_wo[f])
            wv = slot[:, 0:HC * 128].rearrange("p (j m) -> p j m", j=HC)
            po, ro = self.ps()
            self.mmacc(ro, po[:, :TWc], [(wv[:, j, :], ACTH[:, j, :TWc]) for j in range(HC)], [wr, rACTH])
            self.op(dve, lambda: nc.vector.scalar_tensor_tensor(
                out=self.H[:, c, :TWc], in0=po[:, :TWc], scalar=self.GS[:, s, c, b:b + 1],
                in1=self.H[:, c, :TWc], op0=ALU.mult, op1=ALU.add), [ro, self.rGS, self.rH], [self.rH])
            yield

    def final_norm_p(self, TWc, P):
        nc = self.nc
        H, SQ, RS = self.H, P["ACTH"], P["RS"]
        rSQ, rRS = P["rACTH"], P["rRS"]
        self.op(self.act, lambda: nc.scalar.activation(out=SQ[:, 0:4, :TWc], in_=H[:, 0:4, :TWc], func=AF.Square),
                [self.rH], [rSQ])
        self.op(self.dve, lambda: nc.vector.tensor_tensor(out=SQ[:, 4:8, :TWc], in0=H[:, 4:8, :TWc],
                                                          in1=H[:, 4:8, :TWc], op=ALU.mult), [self.rH], [rSQ])
        pt, pr = self.ps()
        self.mmacc(pr, pt[:, :TWc], [(self.ONES[:, :], SQ[:, c, :TWc]) for c in range(KC)], [self.rONES, rSQ])
        self.op(self.act, lambda: nc.scalar.activation(out=RS[:, :TWc], in_=pt[:, :TWc], func=AF.Ln,
                                                       bias=self.EPSB[:, 0:1], scale=1.0 / D), [pr, self.rEPSB], [rRS])
        self.op(self.act, lambda: nc.scalar.activation(out=RS[:, :TWc], in_=RS[:, :TWc], func=AF.Exp, scale=-0.5),
                [rRS], [rRS])
        for c in range(KC):
            self.op(self.dve, lambda: nc.vector.scalar_tensor_tensor(
                out=H[:, c, :TWc], in0=H[:, c, :TWc], scalar=self.vcol("fnw", c), in1=RS[:, :TWc],
                op0=ALU.mult, op1=ALU.mult), [self.rH, rRS, self.rVECS], [self.rH])

    def ffn2_task(self, b, t0, P):
        TW = self.TW
        self.dma(self.pool, self.chH, self.H[:, :, :TW], self.h2s[b][:, :, t0:t0 + TW],
                 [self.dr(("h2s", b, t0))], [self.rH])
        yield from self.ffn_gen(1, 2, b, TW, P)
        self.final_norm_p(TW, P)
        self.dma(self.pool, self.chHst, self.outT[b][:, :, t0:t0 + TW], self.H[:, :, :TW], [self.rH],
                 [self.dr(("out", b, t0))])
        yield

    def pass1(self, b, tiles, M0, prev):
        TW = self.TW
        with ExitStack() as stp:
            P = {"UF": self.sb("p_uf", [128, KC, 512], BF16, stp),
                 "ACTH": self.sb("p_acth", [128, HC, 512], BF16, stp),
                 "SG": [self.sb("p_sg%d" % i, [128, 512], F32, stp) for i in range(2)],
                 "TMP": [self.sb("p_tmp%d" % i, [128, 512], F32, stp) for i in range(2)],
                 "RS": self.sb("p_rs", [128, 512], F32, stp),
                 "rUF": ResGroup(KC), "rACTH": Res(), "rSG": [Res(), Res()], "rTMP": [Res(), Res()], "rRS": Res()}

            def f_head(t0):
                self.dma(self.pool, self.chH, self.H[:, :, :TW], self.xT[b][:, :, t0:t0 + TW], [], [self.rH])
                yield from self.ffn_gen(0, 0, b, TW, P)

            pend = list(prev)

            def ctx_ffn(bn):
                TC = self.TC
                self.dma(self.pool, self.chH, self.H[:, :, :TC], self.cxT[bn], [], [self.rH])
                yield from self.ffn_gen(0, 0, self.NB, TC, P)
                self.dma(self.pool, self.chHst, self.ch1s[bn], self.H[:, :, :TC], [self.rH],
                         [self.dr(("ch1s", bn))])
                yield

            def f_tail(t0):
                self.dma(self.pool, self.chHst, self.h1s[b][:, :, t0:t0 + TW], self.H[:, :, :TW], [self.rH],
                         [self.dr(("h1s", b, t0))])
                self.norm_mod(1, b, TW)

            for i in range(-1, len(tiles)):
                if i < 0:
                    M = M0
                else:
                    t0 = tiles[i]
                    M = self.gla(b, 0, TW, "p1", t0, coro=self.lru(b, 0, TW, "p1", t0, 64))
                def f_stream(i=i):
                    if pend and (i == -1 or i == len(tiles) - 1):
                        pb, pt0 = pend.pop(0)
                        yield from self.ffn2_task(pb, pt0, P)
                    if i + 1 < len(tiles):
                        yield from f_head(tiles[i + 1])
                F = f_stream()
                liveM = liveF = True
                while liveM or liveF:
                    for _ in range(2):
                        if liveM:
                            try:
                                next(M)
                            except StopIteration:
                                liveM = False
                    if liveF:
                        try:
                            next(F)
                        except StopIteration:
                            liveF = False
                if i + 1 < len(tiles):
                    f_tail(tiles[i + 1])
            while pend:
                pb, pt0 = pend.pop(0)
                for _ in self.ffn2_task(pb, pt0, P):
                    pass
            self.barrier()

    def win_block(self, name, j):
        slot, wr = self.wnext(self.s_win[WIN_IDX[(name, j)]], 4096, self.g_win)
        return slot[:, :].rearrange("p (k n) -> p k n", k=KC), wr

    def wp_block(self, w, ob):
        slot, wr = self.wnext(self.s_wp[w * 2 + ob], 4096, self.g_wp)
        return slot[:, :].rearrange("p (k n) -> p k n", k=KC), wr

    def proj_fm(self, wv, wr, cc, TWc):
        pt, pr = self.ps()
        self.mmacc(pr, pt[:, :TWc], [(wv[:, kc, cc * 128:(cc + 1) * 128], self.U[:, kc, :TWc]) for kc in range(KC)],
                   [wr, self.rU])
        return pt, pr

    def proj_tm(self, wv, wr, sub):
        pt, pr = self.ps()
        self.mmacc(pr, pt[:, :], [(self.U[:, kc, sub * 128:(sub + 1) * 128], wv[:, kc, :]) for kc in range(KC)],
                   [wr, self.rU])
        return pt, pr

    def gla(self, b, d, TWc, mode, t0, coro=None):
        nc = self.nc
        pe, act, dve, pool = self.pe, self.act, self.dve, self.pool
        op = self.op
        need_o = mode != "ctx"
        ns = TWc // 128
        CS = self.CONSTS
        TRI = CS[:, CC["tri%d" % d]:CC["tri%d" % d] + 128]
        UTRI = CS[:, CC["utri%d" % d]:CC["utri%d" % d] + 128]
        MASK = CS[:, CC["mask%d" % d]:CC["mask%d" % d] + 128].unsqueeze(1).broadcast_to([128, 4, 128])
        with ExitStack() as st:
            st_a, st_b = ExitStack(), ExitStack()
            sbo = lambda n, sh, dt_: self.sb("g_" + n, sh, dt_, st)
            sba = lambda n, sh, dt_: self.sb("g_" + n, sh, dt_, st_a)
            sb = lambda n, sh, dt_: self.sb("g_" + n, sh, dt_, st_b)
            O = sbo("o", [128, KC, 512], F32) if mode == "p2" else None
            rO = Res()
            V = sba("v", [128, 4, 1024], BF16)
            KEND = sba("kend", [128, 4, 512], BF16)
            QDEC = sba("qdec", [128, 4, 512], BF16) if need_o else None
            KINV = sba("kinv", [128, 4, 512], BF16) if need_o else None
            DEC = sba("dec", [128, 4, 2, 4], F32)
            ST = [sba("st%d" % i, [128, 512], BF16) for i in range(2)]
            rV, rKEND, rQDEC, rKINV, rDEC = Res(), Res(), Res(), Res(), Res()
            QT = sb("qt", [128, 4, 512], F32) if need_o else None
            KT = sb("kt", [128, 4, 512], F32) if need_o else None
            rQT, rKT = Res(), Res()
            EX = sb("ex", [128, 512], F32)
            LB = [sb("lb%d" % i, [128, 512], F32) for i in range(2)]
            EEND = [sb("eend%d" % i, [128, 512], F32) for i in range(2)]
            EB = [sb("eb%d" % i, [128, 512], F32) for i in range(2)]
            EINV = [sb("einv%d" % i, [128, 512], F32) for i in range(2)]
            rEX, rLB, rEEND, rEB, rEINV, rST = Res(), [Res(), Res()], [Res(), Res()], [Res(), Res()], \
                [Res(), Res()], [Res(), Res()]

            order = list(range(ns)) if d == 0 else list(range(ns - 1, -1, -1))
            wkh = {}

            def proj_fd():
                pt, pr = self.ps()
                self.mmacc(pr, pt[0:16, :TWc], [(self.FDW[:, kc, d * 16:(d + 1) * 16], self.U[:, kc, :TWc])
                                                for kc in range(KC)], [self.rFDW, self.rU])
                op(act, lambda: nc.scalar.copy(out=self.FD[0:16, :TWc], in_=pt[0:16, :TWc]), [pr], [self.rFD])

            def proj_q():
                if not need_o:
                    return
                wq, rq = self.win_block("q", 0)
                for h in range(4):
                    pq, prq = self.proj_fm(wq, rq, h, TWc)
                    op(act, lambda h=h, pq=pq: nc.scalar.copy(out=QT[:, h, :TWc], in_=pq[:, :TWc]), [prq], [rQT])

            def proj_k():
                wkh["w"], wkh["r"] = self.win_block("k", 0)
                if need_o:
                    for h in range(4):
                        pk, prk = self.proj_fm(wkh["w"], wkh["r"], h, TWc)
                        op(dve, lambda h=h, pk=pk: nc.vector.tensor_copy(out=KT[:, h, :TWc], in_=pk[:, :TWc]),
                           [prk], [rKT])

            def proj_v(half):
                wv_, rv_ = self.win_block("v", half)
                for sub in range(ns):
                    pv, prv = self.proj_tm(wv_, rv_, sub)
                    if half == 0:
                        op(dve, lambda pv=pv, sub=sub: nc.vector.tensor_copy(out=V[:, sub, 0:512], in_=pv[:, :]),
                           [prv], [rV])
                    else:
                        op(act, lambda pv=pv, sub=sub: nc.scalar.copy(out=V[:, sub, 512:1024], in_=pv[:, :]),
                           [prv], [rV])

            pbs = {}

            def A1(pi):
                sub, k2 = order[pi], pi % 2
                cols = slice(sub * 128, (sub + 1) * 128)
                pz, rz = self.ps()
                self.mmacc(rz, pz[:, :], [(self.FD[0:17, cols], self.FUPA[0:17, d, :])], [self.rFD, self.rFUPA])
                op(act, lambda: nc.scalar.activation(out=EX[:, :], in_=pz[:, :], func=AF.Exp, scale=-1.0),
                   [rz], [rEX])
                op(act, lambda: nc.scalar.activation(out=LB[k2][:, :], in_=EX[:, :], func=AF.Ln, bias=1.0),
                   [rEX], [rLB[k2]])

            def A2(pi):
                sub, k2 = order[pi], pi % 2
                prx, rrx = self.ps()
                self.mmacc(rrx, prx[:, :], [(UTRI, LB[k2][:, :])], [self.rCONSTS, rLB[k2]])
                op(act, lambda: nc.scalar.activation(out=EEND[k2][:, :], in_=prx[:, :], func=AF.Exp),
                   [rrx], [rEEND[k2]])
                pb, rb = self.ps()
                self.mm(rb, [(pb[:, h * 128:(h + 1) * 128], LB[k2][:, h * 128:(h + 1) * 128], TRI, True)
                             for h in range(4)], [self.rCONSTS, rLB[k2]])
                op(act, lambda: nc.scalar.activation(out=EB[k2][:, :], in_=pb[:, :], func=AF.Exp),
                   [rb], [rEB[k2]])
                EB3 = EB[k2][:, :].rearrange("p (h t) -> p h t", h=4)
                for ch in range(2):
                    col = ch * 64 + (63 if d == 0 else 0)
                    op(dve, lambda: nc.vector.tensor_copy(out=DEC[:, sub, ch, :], in_=EB3[:, :, col]),
                       [rEB[k2]], [rDEC])
                if need_o:
                    op(act, lambda: nc.scalar.activation(out=EINV[k2][:, :], in_=pb[:, :], func=AF.Exp,
                                                         scale=-1.0), [rb], [rEINV[k2]])

            def B(pi):
                sub, k2 = order[pi], pi % 2
                cols = slice(sub * 128, (sub + 1) * 128)
                pk, prk = self.proj_tm(wkh["w"], wkh["r"], sub)
                op(dve, lambda: nc.vector.tensor_tensor(
                    out=KEND[:, sub, :], in0=pk[:, :], in1=EEND[k2][:, :], op=ALU.mult), [prk, rEEND[k2]], [rKEND])
                if need_o:
                    EB3 = EB[k2][:, :].rearrange("p (h t) -> p h t", h=4)
                    op(dve, lambda: nc.vector.scalar_tensor_tensor(
                        out=QDEC[:, :, cols], in0=QT[:, :, cols], scalar=float(128 ** -0.5), in1=EB3,
                        op0=ALU.mult, op1=ALU.mult), [rQT, rEB[k2]], [rQDEC])
                    EI3 = EINV[k2][:, :].rearrange("p (h t) -> p h t", h=4)
                    op(dve, lambda: nc.vector.tensor_tensor(
                        out=KINV[:, :, cols], in0=KT[:, :, cols], in1=EI3, op=ALU.mult), [rKT, rEINV[k2]], [rKINV])

            proj_fd()
            A1(0)
            A1(1)
            proj_q()
            A2(0)
            A2(1)
            proj_k()
            B(0)
            B(1)
            if ns == 4:
                A1(2)
                A1(3)
                proj_v(0)
                A2(2)
                A2(3)
                proj_v(1)
                B(2)
                B(3)
            else:
                proj_v(0)
                proj_v(1)
            S32, rS32 = self.S32[d], self.rS32[d]
            co = [0, 1] if d == 0 else [1, 0]
            if need_o:
                nxt = (self.sbf_i + 1) % 3
                self.sbf_i = nxt
                op(act, lambda nxt=nxt: nc.scalar.copy(out=self.SBF[nxt][:, :, :], in_=S32[:, :, :]),
                   [rS32], [self.rSBF[nxt]])
            def scan(si, sub):
                k2 = si % 2
                cols = slice(sub * 128, (sub + 1) * 128)
                if need_o:
                    pss, rss = self.ps()
                    self.mm(rss, [(pss[:, h * 128:(h + 1) * 128], KINV[:, h, cols], QDEC[:, h, cols], True)
                                  for h in range(4)], [rKINV, rQDEC])
                    op(dve, lambda k2=k2, pss=pss: nc.vector.tensor_tensor(
                        out=ST[k2][:, :].rearrange("p (h t) -> p h t", h=4),
                        in0=pss[:, :].rearrange("p (h t) -> p h t", h=4), in1=MASK, op=ALU.mult),
                       [rss, self.rCONSTS], [rST[k2]])
                    sA = self.sbf_i
                yield
                sbufs = [None, None]
                if need_o:
                    sbufs[0] = sA
                for ci, ch in enumerate(co):
                    rows = slice(ch * 64, (ch + 1) * 64)
                    pk0, rk0 = self.ps()
                    pk1, rk1 = self.ps()
                    pks, rks = [pk0, pk1], [rk0, rk1]
                    for hh in range(2):
                        self.mm(rks[hh], [(pks[hh][:, q * 256:(q + 1) * 256],
                                           KEND[rows, sub, (hh * 2 + q) * 128:(hh * 2 + q + 1) * 128],
                                           V[rows, sub, (hh * 2 + q) * 256:(hh * 2 + q + 1) * 256], True)
                                          for q in range(2)], [rKEND, rV])
                    for h in range(4):
                        hh, q = h // 2, h % 2
                        op(dve, lambda h=h, hh=hh, q=q, sub=sub, ch=ch, pks=pks: nc.vector.scalar_tensor_tensor(
                            out=S32[:, h, :], in0=S32[:, h, :], scalar=DEC[:, sub, ch, h:h + 1],
                            in1=pks[hh][:, q * 256:(q + 1) * 256], op0=ALU.mult, op1=ALU.add),
                            [rS32.parts[h], rDEC, rks[hh]], [rS32.parts[h]])
                    if need_o:
                        nxt = (self.sbf_i + 1) % 3
                        self.sbf_i = nxt
                        op(act, lambda nxt=nxt: nc.scalar.copy(out=self.SBF[nxt][:, :, :], in_=S32[:, :, :]),
                           [rS32], [self.rSBF[nxt]])
                        if ci == 0:
                            sbufs[1] = nxt
                    yield
                if not need_o:
                    return
                banks = [self.ps(), self.ps()]
                for g in range(8):
                    h, ec = g // 2, g % 2
                    pt_, pr_ = banks[g // 4]
                    oc = (g % 4) * 128
                    items = [(pt_[:, oc:oc + 128], V[:, sub, h * 256 + ec * 128:h * 256 + (ec + 1) * 128],
                              ST[k2][:, h * 128:(h + 1) * 128], True)]
                    for ci, ch in enumerate(co):
                        sbi = sbufs[ci]
                        items.append((pt_[:, oc + ch * 64:oc + (ch + 1) * 64],
                                      self.SBF[sbi][:, h, ec * 128:(ec + 1) * 128],
                                      QDEC[:, h, sub * 128 + ch * 64:sub * 128 + (ch + 1) * 64], False))
                    self.mm(pr_, items, [rV, rST[k2], self.rSBF[sbufs[0]], self.rSBF[sbufs[1]], rQDEC])
                tcols = slice(t0 + sub * 128, t0 + (sub + 1) * 128)
                if mode == "p1":
                    kb = 0
                    for gq in range(2):
                        pt_, pr_ = banks[gq]
                        src3 = pt_[:, :].rearrange("p (g t) -> p g t", g=4)
                        op(act, lambda kb=kb, gq=gq, src3=src3: nc.scalar.copy(
                            out=self.OB[kb][:, gq * 4:(gq + 1) * 4, :], in_=src3), [pr_], [self.rOB[kb]])
                    self.dma(pool, self.chOB[kb], self.ofs[b][:, :, tcols], self.OB[kb][:, :, :],
                             [self.rOB[kb]], [self.dr(("ofs", b, t0, sub))])
                else:
                    kb = 0
                    self.dma(pool, self.chOFL[kb], self.OFL[kb][:, :, :], self.ofs[b][:, :, tcols],
                             [self.dr(("ofs", b, t0, sub))], [self.rOFL[kb]])
                    for gq in range(2):
                        pt_, pr_ = banks[gq]
                        src3 = pt_[:, :].rearrange("p (g t) -> p g t", g=4)
                        op(dve, lambda kb=kb, gq=gq, src3=src3, cols=cols: nc.vector.tensor_tensor(
                            out=O[:, gq * 4:(gq + 1) * 4, cols], in0=src3, in1=self.OFL[kb][:, gq * 4:(gq + 1) * 4, :],
                            op=ALU.add), [pr_, self.rOFL[kb]], [rO])
                yield

            self.barrier()
            st_b.close()

            def scan_all():
                for si, sub in enumerate(order):
                    yield from scan(si, sub)
            g1, g2 = scan_all(), (coro if coro is not None else iter(()))
            live1 = live2 = True
            while live1 or live2:
                if live1:
                    try:
                        next(g1)
                    except StopIteration:
                        live1 = False
                    yield
                for _ in range(2):
                    if live2:
                        try:
                            next(g2)
                        except StopIteration:
                            live2 = False
                        yield
            self.barrier()
            st_a.close()
            if mode == "p2":
                self.gla_post(b, TWc, O, rO, st)
            self.barrier()

    def gla_post(self, b, TWc, O, rO, st):
        nc = self.nc
        act, dve = self.act, self.dve
        op = self.op
        SQ2 = self.sb("gp_sq", [128, KC, 512], BF16, st)
        RS = [self.sb("gp_rs%d" % h, [128, 512], F32, st) for h in range(4)]
        rSQ2, rRS = Res(), [Res() for _ in range(4)]
        OG = self.sb("gp_og", [128, KC, 512], BF16, st)
        rOG = Res()
        T1 = [self.sb("gp_t1%d" % i, [128, 512], F32, st) for i in range(2)]
        SGT = [self.sb("gp_sg%d" % i, [128, 512], F32, st) for i in range(2)]
        rT1, rSGT = [Res(), Res()], [Res(), Res()]
        op(act, lambda: nc.scalar.activation(out=SQ2[:, :, :TWc], in_=O[:, :, :TWc], func=AF.Square), [rO], [rSQ2])
        for h in range(4):
            pt, pr = self.ps()
            self.mmacc(pr, pt[:, :TWc], [(self.ONES[:, :], SQ2[:, 2 * h + e, :TWc]) for e in range(2)],
                       [self.rONES, rSQ2])
            op(act, lambda h=h, pt=pt: nc.scalar.activation(out=RS[h][:, :TWc], in_=pt[:, :TWc], func=AF.Ln,
                                                            bias=self.EPSB[:, 0:1], scale=1.0 / 256),
               [pr, self.rEPSB], [rRS[h]])
            op(act, lambda h=h: nc.scalar.activation(out=RS[h][:, :TWc], in_=RS[h][:, :TWc], func=AF.Exp, scale=-0.5),
               [rRS[h]], [rRS[h]])
        for gb in range(2):
            wg, rg = self.win_block("g", gb)
            for cc in range(4):
                c = gb * 4 + cc
                h, ec, k = c // 2, c % 2, c % 2
                pg, prg = self.proj_fm(wg, rg, cc, TWc)
                op(act, lambda k=k, pg=pg: nc.scalar.activation(out=SGT[k][:, :TWc], in_=pg[:, :TWc], func=AF.Silu),
                   [prg], [rSGT[k]])
                op(dve, lambda k=k, c=c, h=h: nc.vector.tensor_tensor(
                    out=T1[k][:, :TWc], in0=O[:, c, :TWc], in1=RS[h][:, :TWc], op=ALU.mult), [rO, rRS[h]], [rT1[k]])
                op(dve, lambda k=k, c=c, ec=ec: nc.vector.scalar_tensor_tensor(
                    out=OG[:, c, :TWc], in0=T1[k][:, :TWc], scalar=self.vcol("gnw", ec), in1=SGT[k][:, :TWc],
                    op0=ALU.mult, op1=ALU.mult), [rT1[k], rSGT[k], self.rVECS], [rOG])
        for ob in range(2):
            w, rw = self.wp_block(0, ob)
            wga, rga = self.win_block("ga", ob)
            for cc in range(4):
                c = ob * 4 + cc
                k = c % 2
                py, pry = self.ps()
                self.mmacc(pry, py[:, :TWc], [(w[:, kc, cc * 128:(cc + 1) * 128], OG[:, kc, :TWc])
                                              for kc in range(KC)], [rw, rOG])
                pa, pra = self.proj_fm(wga, rga, cc, TWc)
                op(act, lambda k=k, pa=pa: nc.scalar.activation(out=SGT[k][:, :TWc], in_=pa[:, :TWc],
                                                                func=AF.Tanh, scale=0.5), [pra], [rSGT[k]])
                op(dve, lambda k=k, c=c, py=py: nc.vector.scalar_tensor_tensor(
                    out=self.M1[:, c, :TWc], in0=SGT[k][:, :TWc], scalar=1.0, in1=py[:, :TWc],
                    op0=ALU.add, op1=ALU.mult), [rSGT[k], pry], [self.rM1])

    def lru(self, b, d, TWc, mode, t0, RL):
        nc = self.nc
        pe, act, dve, pool = self.pe, self.act, self.dve, self.pool
        op = self.op
        with ExitStack() as st:
            sb = lambda n, sh, dt_: self.sb("l_" + n, sh, dt_, st)
            XL = [sb("xl%d" % i, [128, 512], F32) for i in range(4)]
            XC = [sb("xc%d" % i, [128, 512], F32) for i in range(2)]
            XCB = [sb("xcb%d" % i, [128, 512], BF16) for i in range(2)]
            TR = [sb("tr%d" % i, [128, 512], F32) for i in range(2)]
            TI = [sb("ti%d" % i, [128, 512], F32) for i in range(2)]
            AA = [sb("a%d" % i, [128, 512], F32) for i in range(2)]
            E2 = [sb("e2%d" % i, [128, 512], F32) for i in range(2)]
            UU = [sb("u%d" % i, [128, 512], F32) for i in range(2)]
            GL = [sb("gl%d" % i, [128, 512], F32) for i in range(2)] if mode == "p2" else None
            rl = lambda n: [Res() for _ in range(n)]
            rXL, rXC, rXCB, rTR, rTI, rAA, rE2, rUU, rGL = rl(4), rl(2), rl(2), rl(2), rl(2), rl(2), rl(2), rl(2), rl(2)
            CAR, rCAR = self.CARRY[d], self.rCARRY[d]
            for xb in range(2):
                wx, rwx = self.win_block("xl", xb)
                for cc in range(4):
                    px, prx = self.proj_fm(wx, rwx, cc, TWc)
                    op(act, lambda: nc.scalar.copy(out=XL[cc][:, :TWc], in_=px[:, :TWc]), [prx], [rXL[cc]])
                yield
                wy = rwy = None

                def a1(cc, k):
                    n = xb * 4 + cc
                    op(dve, lambda: nc.vector.tensor_scalar(
                        out=XC[k][:, :TWc], in0=XL[cc][:, :TWc], scalar1=self.vcol("cw", 2 * 8 + n),
                        scalar2=self.vcol("cb", n), op0=ALU.mult, op1=ALU.add), [rXL[cc], self.rVECS], [rXC[k]])
                    xl3 = XL[cc][:, :TWc].rearrange("p (r t) -> p r t", t=RL)
                    xc3 = XC[k][:, :TWc].rearrange("p (r t) -> p r t", t=RL)
                    for j, o in ((0, -2), (1, -1), (3, 1)):
                        dsl = slice(max(0, -o), RL - max(0, o))
                        ssl = slice(max(0, o), RL - max(0, -o))
                        op(dve, lambda: nc.vector.scalar_tensor_tensor(
                            out=xc3[:, :, dsl], in0=xl3[:, :, ssl], scalar=self.vcol("cw", j * 8 + n),
                            in1=xc3[:, :, dsl], op0=ALU.mult, op1=ALU.add), [rXL[cc], rXC[k], self.rVECS], [rXC[k]])
                    op(act, lambda: nc.scalar.copy(out=XCB[k][:, :TWc], in_=XC[k][:, :TWc]), [rXC[k]], [rXCB[k]])

                def a2(cc, k):
                    n = xb * 4 + cc
                    q = k
                    pr_, rr_ = self.ps()
                    self.mmacc(rr_, pr_[:, :TWc], [(self.LRUW[:, (0 * 2 + d) * 8 + n, :], XCB[k][:, :TWc])],
                               [self.rLRUW, rXCB[k]])
                    pi_, ri_ = self.ps()
                    self.mmacc(ri_, pi_[:, :TWc], [(self.LRUW[:, (1 * 2 + d) * 8 + n, :], XCB[k][:, :TWc])],
                               [self.rLRUW, rXCB[k]])
                    dn = d * 8 + n
                    op(act, lambda: nc.scalar.activation(out=TR[k][:, :TWc], in_=pr_[:, :TWc], func=AF.Tanh,
                                                         bias=self.HBIAS[:, dn:dn + 1], scale=0.5),
                       [rr_, self.rNEGC], [rTR[k]])
                    op(act, lambda: nc.scalar.activation(out=TI[k][:, :TWc], in_=pi_[:, :TWc], func=AF.Tanh,
                                                         bias=self.HBIAS[:, 16 + dn:16 + dn + 1], scale=0.5),
                       [ri_, self.rNEGC], [rTI[k]])
                    op(act, lambda: nc.scalar.activation(out=AA[q][:, :TWc], in_=TR[k][:, :TWc], func=AF.Exp,
                                                         bias=self.NEGC2[:, dn:dn + 1], scale=self.NEGC2[:, dn:dn + 1]),
                       [rTR[k], self.rNEGC], [rAA[q]])
                    op(act, lambda: nc.scalar.activation(out=E2[q][:, :TWc], in_=TR[k][:, :TWc], func=AF.Exp,
                                                         bias=self.NEGC[:, dn:dn + 1], scale=self.NEGC[:, dn:dn + 1]),
                       [rTR[k], self.rNEGC], [rE2[q]])
                    op(act, lambda: nc.scalar.activation(out=E2[q][:, :TWc], in_=E2[q][:, :TWc], func=AF.Relu,
                                                         bias=self.EPSB[:, 2:3], scale=-0.25),
                       [rE2[q], self.rEPSB], [rE2[q]])
                    op(dve, lambda: nc.vector.scalar_tensor_tensor(
                        out=UU[q][:, :TWc], in0=TI[k][:, :TWc], scalar=1.0, in1=XC[k][:, :TWc],
                        op0=ALU.add, op1=ALU.mult), [rTI[k], rXC[k]], [rUU[q]])

                def b1(cc, q):
                    n = xb * 4 + cc
                    op(act, lambda: nc.scalar.activation(out=E2[q][:, :TWc], in_=E2[q][:, :TWc], func=AF.Sqrt),
                       [rE2[q]], [rE2[q]])
                    op(dve, lambda: nc.vector.tensor_tensor(out=UU[q][:, :TWc], in0=UU[q][:, :TWc],
                                                            in1=E2[q][:, :TWc], op=ALU.mult),
                       [rUU[q], rE2[q]], [rUU[q]])
                    kh = self.cnt2 % 2
                    self.cnt2 += 1
                    HS, rHS = self.HSC[kh], self.rHSC[kh]
                    if d == 0:
                        o_ap, a_ap, u_ap = HS[:, 0:TWc], AA[q][:, 0:TWc], UU[q][:, 0:TWc]
                        last = TWc - 1
                    else:
                        rv = slice(TWc - 1, None, -1)
                        o_ap, a_ap, u_ap = HS[:, rv], AA[q][:, rv], UU[q][:, rv]
                        last = 0
                    op(dve, lambda: nc.vector.tensor_tensor_scan(
                        out=o_ap, data0=a_ap, data1=u_ap, initial=CAR[:, n:n + 1], op0=ALU.mult, op1=ALU.add),
                        [rAA[q], rUU[q], rCAR], [rHS])
                    op(dve, lambda: nc.vector.tensor_copy(out=CAR[:, n:n + 1], in_=HS[:, last:last + 1]),
                       [rHS], [rCAR])
                    if mode == "p1":
                        self.dma(pool, self.chHSC[kh], self.hfs[b][:, n, t0:t0 + TWc], HS[:, :TWc],
                                 [rHS], [self.dr(("hfs", b, t0, n))])
                    elif mode == "p2":
                        self.dma(pool, self.chHFLC[kh], self.HFLC[kh][:, :TWc], self.hfs[b][:, n, t0:t0 + TWc],
                                 [self.dr(("hfs", b, t0, n))], [self.rHFLC[kh]])
                        op(dve, lambda: nc.vector.tensor_tensor(
                            out=HS[:, :TWc], in0=HS[:, :TWc], in1=self.HFLC[kh][:, :TWc], op=ALU.add),
                            [rHS, self.rHFLC[kh]], [rHS])
                    return HS, rHS

                def c1(cc, k, HS, rHS):
                    n = xb * 4 + cc
                    py, pry = self.proj_fm(wy, rwy, cc, TWc)
                    op(act, lambda: nc.scalar.activation(out=GL[k][:, :TWc], in_=py[:, :TWc],
                                                         func=AF.Gelu_apprx_tanh), [pry], [rGL[k]])
                    op(dve, lambda: nc.vector.tensor_tensor(
                        out=self.HL[:, n, :TWc], in0=HS[:, :TWc], in1=GL[k][:, :TWc], op=ALU.mult),
                        [rHS, rGL[k]], [self.rHL])

                for pair in range(2):
                    a1(pair * 2, 0)
                    yield
                    a1(pair * 2 + 1, 1)
                    yield
                    a2(pair * 2, 0)
                    yield
                    a2(pair * 2 + 1, 1)
                    yield
                    hs = []
                    for i in range(2):
                        hs.append(b1(pair * 2 + i, i))
                        yield
                    if mode == "p2":
                        if wy is None:
                            wy, rwy = self.win_block("yl", xb)
                        for i in range(2):
                            c1(pair * 2 + i, i, *hs[i])
                            yield

    def mix_out(self, b, TWc):
        nc = self.nc
        act, dve = self.act, self.dve
        op = self.op
        with ExitStack() as st:
            M2 = self.sb("mo_m2", [128, KC, 512], BF16, st)
            SGT = [self.sb("mo_sg%d" % i, [128, 512], F32, st) for i in range(2)]
            rM2, rSGT = Res(), [Res(), Res()]
            for ob in range(2):
                w, rw = self.wp_block(1, ob)
                wgb, rgb = self.win_block("gb", ob)
                for cc in range(4):
                    c = ob * 4 + cc
                    k = c % 2
                    py, pry = self.ps()
                    self.mmacc(pry, py[:, :TWc], [(w[:, kc, cc * 128:(cc + 1) * 128], self.HL[:, kc, :TWc])
                                                  for kc in range(KC)], [rw, self.rHL])
                    pb, prb = self.proj_fm(wgb, rgb, cc, TWc)
                    op(act, lambda k=k, pb=pb: nc.scalar.activation(out=SGT[k][:, :TWc], in_=pb[:, :TWc],
                                                                    func=AF.Tanh, scale=0.5), [prb], [rSGT[k]])
                    op(dve, lambda k=k, c=c, py=py: nc.vector.scalar_tensor_tensor(
                        out=M2[:, c, :TWc], in0=SGT[k][:, :TWc], scalar=1.0, in1=py[:, :TWc],
                        op0=ALU.add, op1=ALU.mult), [rSGT[k], pry], [rM2])
            for ob in range(2):
                w, rw = self.wp_block(2, ob)
                for cc in range(4):
                    c = ob * 4 + cc
                    po, pro = self.ps()
                    pairs = [(w[:, kc, cc * 128:(cc + 1) * 128], self.M1[:, kc, :TWc]) for kc in range(KC)] + \
                            [(w[:, kc, cc * 128:(cc + 1) * 128], M2[:, kc, :TWc]) for kc in range(KC)]
                    self.mmacc(pro, po[:, :TWc], pairs, [rw, self.rM1, rM2])
                    op(dve, lambda c=c, po=po: nc.vector.scalar_tensor_tensor(
                        out=self.H[:, c, :TWc], in0=po[:, :TWc], scalar=self.GS[:, 1, c, b:b + 1],
                        in1=self.H[:, c, :TWc], op0=ALU.mult, op1=ALU.add), [pro, self.rGS, self.rH], [self.rH])
            self.barrier()

    def final_norm(self, TWc):
        nc = self.nc
        with ExitStack() as st:
            RS, rRS = self.norm_stats(st, TWc, KC, self.H[:, :, :TWc], self.rH, 1.0 / D, "fn")
            for c in range(KC):
                self.op(self.dve, lambda c=c: nc.vector.scalar_tensor_tensor(
                    out=self.HOUT[:, c, :TWc], in0=self.H[:, c, :TWc], scalar=self.vcol("fnw", c), in1=RS[:, :TWc],
                    op0=ALU.mult, op1=ALU.mult), [self.rH, rRS, self.rVECS], [self.rM1, self.rHL])
            self.barrier()

    def batch(self, b):
        nc, NB, T, TC, TW = self.nc, self.NB, self.T, self.TC, self.TW
        pool, dve = self.pool, self.dve
        op, dma = self.op, self.dma
        if b == 0:
            self.EPSB = self.sb("epsb", [128, 3], F32)
            self.ONEB = self.EPSB[:, 1:2]
            self.rEPSB = Res()
            op(dve, lambda: nc.vector.memset(self.EPSB[:, 0:1], EPS), [], [self.rEPSB])
            op(dve, lambda: nc.vector.memset(self.EPSB[:, 1:2], 1.0), [], [self.rEPSB])
            op(dve, lambda: nc.vector.memset(self.EPSB[:, 2:3], 0.25), [], [self.rEPSB])
        for d in range(2):
            op(dve, lambda d=d: nc.vector.memset(self.S32[d][:, :, :], 0.0), [], [self.rS32[d]])
            op(dve, lambda d=d: nc.vector.memset(self.CARRY[d][:, :], 0.0), [], [self.rCARRY[d]])
        if True:
            dma(pool, self.chH, self.H[:, :, :TC], self.cxT[b], [], [self.rH])
            self.ffn(0, 0, NB, TC)
        else:
            dma(pool, self.chH, self.H[:, :, :TC], self.ch1s[b], [self.dr(("ch1s", b))], [self.rH])
        self.norm_mod(1, NB, TC)
        def m_ctx():
            for d in range(2):
                yield from self.gla(b, d, TC, "ctx", 0, coro=self.lru(b, d, TC, "ctx", 0, TC))
        tiles = list(range(0, T, TW))
        self.pass1(b, tiles, m_ctx(), self.deferred)
        self.deferred = []
        with ExitStack() as st2:
            MH = self.sb("mh_sb", [128, 2 * KC * 512], BF16, st2)
            self.M1 = MH[:, 0:KC * 512].rearrange("p (c t) -> p c t", c=KC)
            self.HL = MH[:, KC * 512:2 * KC * 512].rearrange("p (c t) -> p c t", c=KC)
            self.HOUT = MH[:, :].bitcast(F32).rearrange("p (c t) -> p c t", c=KC)
            for t0 in reversed(tiles):
                dma(pool, self.chH, self.H[:, :, :TW], self.h1s[b][:, :, t0:t0 + TW], [self.dr(("h1s", b, t0))],
                    [self.rH])
                self.norm_mod(1, b, TW)
                for _ in self.gla(b, 1, TW, "p2", t0, coro=self.lru(b, 1, TW, "p2", t0, 64)):
                    pass
                self.mix_out(b, TW)
                if b + 1 < NB and t0 in tiles[:2] and len(tiles) >= 4:
                    dma(pool, self.chH2, self.h2s[b][:, :, t0:t0 + TW], self.H[:, :, :TW], [self.rH],
                        [self.dr(("h2s", b, t0))])
                    self.deferred.append((b, t0))
                    continue
                self.ffn(1, 2, b, TW)
                self.final_norm(TW)
                dma(pool, self.chHst, self.outT[b][:, :, t0:t0 + TW], self.HOUT[:, :, :TW], [self.rM1, self.rHL],
                    [self.dr(("out", b, t0))])
            for e in (self.pe, self.act, self.dve, self.pool):
                e.b.wait_ge(self.chHst.sem, self.chHst.cnt)
                e.seen[self.chHst] = self.chHst.cnt
            self.barrier()


def _fm(v):
    v = np.asarray(v, np.float32)
    return np.ascontiguousarray(v.reshape(-1, 128).T)


def _consts():
    c = np.zeros((128, NCONST), np.float32)
    j = np.arange(128)[:, None]
    i = np.arange(128)[None, :]
    same = (j // 64) == (i // 64)
    c[:, 0:128] = np.where(same & (j <= i), -1.0 / 16, 0.0)
    c[:, 128:256] = np.where(same & (j >= i), -1.0 / 16, 0.0)
    c[:, 256:384] = np.where(same & (j > i), -1.0 / 16, 0.0)
    c[:, 384:512] = np.where(same & (j < i), -1.0 / 16, 0.0)
    m0 = np.where(same & (j <= i), 1.0, 0.0)
    m1 = np.where(same & (j > i), 1.0, 0.0)
    c[:, 512:640] = m0
    c[:, 640:768] = m1
    return c


def _to_fm_tokens(a):
    nb, t, _ = a.shape
    return np.ascontiguousarray(a.reshape(nb, t, KC, 128).transpose(0, 3, 2, 1))


def _from_fm_tokens(a):
    nb, _, _, t = a.shape
    return np.ascontiguousarray(a.transpose(0, 3, 2, 1).reshape(nb, t, D))


_NC_CACHE = {}


def run(inputs, n_cores, trace=False):
    x = np.asarray(inputs["x"], np.float32)
    ctx = np.asarray(inputs["ctx"], np.float32)
    c = np.asarray(inputs["c"], np.float32)
    B, T, _ = x.shape
    TC = ctx.shape[1]
    NB = B // n_cores
    key = (NB, T, TC)
    prog = Prog(NB, T, TC)
    nc = prog.build()
    g = lambda k: np.asarray(inputs[k], np.float32)
    vecs = np.zeros((128, NVEC), np.float32)

    def put(name, arr):
        a = _fm(arr)
        vecs[:, VC[name]:VC[name] + a.shape[1]] = a
    put("nw", g("norm_w")[0])
    put("bada", g("b_ada")[0])
    put("cw", g("conv_w")[0])
    put("cb", g("conv_b")[0])
    put("lbr", g("lru_br")[0])
    put("lbi", g("lru_bi")[0])
    put("lam", g("lru_lam")[0])
    put("gnw", g("gla_norm_w")[0])
    put("fnw", g("final_norm_w"))
    fupa = np.zeros((32, 2, 512), np.float32)
    fupa[0:16] = g("gla_fup")[0].transpose(1, 0, 2)
    fupa[16] = g("gla_fb")[0]
    shared = {
        "vecs": vecs, "consts": _consts(), "fupa": fupa,
        "w_ada": np.ascontiguousarray(g("w_ada")[0]),
        "ffn1_wi": np.ascontiguousarray(g("ffn1_wi")[0]), "ffn1_wo": np.ascontiguousarray(g("ffn1_wo")[0]),
        "ffn2_wi": np.ascontiguousarray(g("ffn2_wi")[0]), "ffn2_wo": np.ascontiguousarray(g("ffn2_wo")[0]),
        "w_in": np.ascontiguousarray(g("w_in")[0]),
        "lru_wr": np.ascontiguousarray(g("lru_wr")[0]), "lru_wi": np.ascontiguousarray(g("lru_wi")[0]),
        "w_out_gla": np.ascontiguousarray(g("w_out_gla")[0]), "w_out_lru": np.ascontiguousarray(g("w_out_lru")[0]),
        "w_o": np.ascontiguousarray(g("w_o")[0]),
    }
    c_ctx = g("c_ctx")
    in_maps = []
    for i in range(n_cores):
        sl = slice(i * NB, (i + 1) * NB)
        cc = np.concatenate([c[sl], c_ctx[None, :]], axis=0)
        cT = np.ascontiguousarray(cc.reshape(NB + 1, KC, 128).transpose(2, 1, 0))
        m = dict(shared)
        m["xT"] = _to_fm_tokens(x[sl])
        m["cxT"] = _to_fm_tokens(ctx[sl])
        m["cT"] = cT
        in_maps.append(m)
    res = run_bass_kernel_spmd(nc, in_maps, core_ids=list(range(n_cores)), trace=trace)
    out = np.concatenate([_from_fm_tokens(np.asarray(r["outT"])) for r in res.results], axis=0)
    return out.astype(np.float32), res


def kernel(**inputs):
    out, _ = run(inputs, 8)
    return out
```
